# Optimizing a Trainium2 kernel written in Bass

```python
import jax, jax.numpy as jnp
from jax import lax
import numpy as np

D_MODEL = 4096
BATCH = 4
SEQ = 4096
DEPTH = 2

CTX_LEN = 256
GRID_W = 64
CHUNK = 64
CONV_W = 4
D_RNN = D_MODEL // 4
RNN_BLOCKS = 8
LRU_C = 8.0
GLA_HEADS = 4
GLA_DK = D_MODEL // 8 // GLA_HEADS
GLA_DV = D_MODEL // 4 // GLA_HEADS
GLA_RANK = 16
GLA_TAU = 16.0
MLSTM_HEADS = 4
MLSTM_D = D_MODEL // 4 // MLSTM_HEADS
N_BRANCH = 3
N_EXPERTS = 16
EC_CAPACITY = 2
MOE_FF = D_MODEL // 4
DEEPNORM_ALPHA = (2 * DEPTH) ** 0.25
DEEPNORM_BETA = (8 * DEPTH) ** -0.25
LN_EPS = 1e-5
GLA_K = GLA_HEADS * GLA_DK
GLA_V = GLA_HEADS * GLA_DV
MLSTM_W = MLSTM_HEADS * MLSTM_D
IN_SIZES = (D_RNN, D_RNN, GLA_K, GLA_K, GLA_V, GLA_V, 2 * GLA_RANK, MLSTM_W, MLSTM_W, MLSTM_W, MLSTM_W, 4 * MLSTM_HEADS)
N_FEAT = sum(IN_SIZES)
N_IN = N_FEAT + N_BRANCH * D_MODEL
IN_SPLIT = tuple(sum(IN_SIZES[:i + 1]) for i in range(len(IN_SIZES) - 1))

kernel_name = 'hybrid_rglru_gla_mlstm_ec_moe_dit_trunk'


def layer_norm(t, g, b):
    tf = t.astype(jnp.float32)
    mu = tf.mean(-1, keepdims=True)
    var = jnp.square(tf - mu).mean(-1, keepdims=True)
    return ((tf - mu) * lax.rsqrt(var + LN_EPS) * g + b).astype(t.dtype)


def head_norm(t, g):
    mu = t.mean(-1, keepdims=True)
    var = jnp.square(t - mu).mean(-1, keepdims=True)
    y = (t - mu) * lax.rsqrt(var + LN_EPS)
    return y.reshape(t.shape[0], t.shape[1], -1) * g


def centred_dwconv(t, w, b):
    k = w.shape[0]
    left = (k - 1) // 2
    y = lax.conv_general_dilated(t, w[:, None, :].astype(t.dtype), (1,), [(left, k - 1 - left)],
                                 dimension_numbers=('NWC', 'WIO', 'NWC'), feature_group_count=t.shape[-1])
    return y + b


def to_colmajor(t):
    bn, n, w = t.shape
    rows = n // GRID_W
    return t.reshape(bn, rows, GRID_W, w).transpose(0, 2, 1, 3).reshape(bn, n, w)


def from_colmajor(t):
    bn, n, w = t.shape
    rows = n // GRID_W
    return t.reshape(bn, GRID_W, rows, w).transpose(0, 2, 1, 3).reshape(bn, n, w)


def flip_parts(t, nc):
    return jnp.concatenate([jnp.flip(t[:, :nc], 1), jnp.flip(t[:, nc:], 1)], axis=1)


def block_diag(u, w):
    nb, bs, _ = w.shape
    return jnp.einsum('btni,nij->btnj', u.reshape(u.shape[0], u.shape[1], nb, bs), w).reshape(u.shape)


def linear_scan(a, b):
    def comb(l, r):
        return (l[0] * r[0], r[0] * l[1] + r[1])
    return lax.associative_scan(comb, (a, b), axis=1)[1]


def to_chunks(t):
    bn, n, h, d = t.shape
    return t.reshape(bn, n // CHUNK, CHUNK, h, d).transpose(1, 0, 3, 2, 4)


def from_chunks(t):
    nch, bn, h, l, d = t.shape
    return t.transpose(1, 0, 3, 2, 4).reshape(bn, nch * l, h, d)


def gla_chunked(q, k, v, g):
    bn, _, h, dk = q.shape
    dv = v.shape[-1]
    mask = jnp.tril(jnp.ones((CHUNK, CHUNK), bool))

    def step(s, inp):
        qi, ki, vi, gi = inp
        b = jnp.cumsum(gi, axis=2)
        b_mid = b[:, :, CHUNK // 2 - 1:CHUNK // 2]
        b_last = b[:, :, -1:]
        att = jnp.einsum('bhid,bhjd->bhij', qi * jnp.exp(b - b_mid), ki * jnp.exp(b_mid - b))
        att = jnp.where(mask, att, 0.0)
        o = jnp.einsum('bhij,bhjv->bhiv', att, vi) + jnp.einsum('bhid,bhdv->bhiv', qi * jnp.exp(b), s)
        s = jnp.exp(b_last[:, :, 0])[..., None] * s + jnp.einsum('bhjd,bhjv->bhdv', ki * jnp.exp(b_last - b), vi)
        return s, o

    s0 = jnp.zeros((bn, h, dk, dv), jnp.float32)
    _, o = lax.scan(step, s0, (to_chunks(q), to_chunks(k), to_chunks(v), to_chunks(g)))
    return from_chunks(o)


def mlstm_chunked(q, k, v, ig, lf):
    bn, _, h, d = q.shape
    mask = jnp.tril(jnp.ones((CHUNK, CHUNK), bool))

    def step(carry, inp):
        cm, nv, m = carry
        qi, ki, vi, ii, fi = inp
        b = jnp.cumsum(fi, axis=-1)
        dmat = jnp.where(mask, b[..., :, None] - b[..., None, :] + ii[..., None, :], -jnp.inf)
        inter = b + m[..., None]
        m_row = jnp.maximum(dmat.max(-1), inter)
        p = jnp.einsum('bhid,bhjd->bhij', qi, ki) * jnp.exp(dmat - m_row[..., None])
        s_inter = jnp.exp(inter - m_row)
        num = jnp.einsum('bhij,bhjv->bhiv', p, vi) + s_inter[..., None] * jnp.einsum('bhvd,bhid->bhiv', cm, qi)
        den = p.sum(-1) + s_inter * jnp.einsum('bhd,bhid->bhi', nv, qi)
        hout = num / jnp.maximum(jnp.abs(den), jnp.exp(-m_row))[..., None]
        b_last = b[..., -1]
        wl = b_last[..., None] - b + ii
        m_new = jnp.maximum(b_last + m, wl.max(-1))
        sw = jnp.exp(wl - m_new[..., None])
        decay = jnp.exp(b_last + m - m_new)
        cm = decay[..., None, None] * cm + jnp.einsum('bhj,bhjv,bhjd->bhvd', sw, vi, ki)
        nv = decay[..., None] * nv + jnp.einsum('bhj,bhjd->bhd', sw, ki)
        return (cm, nv, m_new), hout

    carry0 = (jnp.zeros((bn, h, d, d), jnp.float32), jnp.zeros((bn, h, d), jnp.float32), jnp.zeros((bn, h), jnp.float32))
    gates = lambda t: to_chunks(t[..., None])[..., 0]
    _, o = lax.scan(step, carry0, (to_chunks(q), to_chunks(k), to_chunks(v), gates(ig), gates(lf)))
    return from_chunks(o)


def rglru_branch(a_x, a_g, nc, r0, conv_w, conv_b, wa, ba, wx, bx, lam):
    u = jnp.concatenate([centred_dwconv(a_x[:, :nc], conv_w, conv_b),
                         centred_dwconv(to_colmajor(a_x[:, nc:]), conv_w, conv_b)], axis=1).astype(jnp.float32)
    h = 0.0
    for d in range(2):
        r = jax.nn.sigmoid(block_diag(u, wa[d]) + ba[d])
        i = jax.nn.sigmoid(block_diag(u, wx[d]) + bx[d])
        log_a = -LRU_C * jax.nn.softplus(-lam[d]) * r
        a = jnp.exp(log_a)
        bt = jnp.sqrt(-jnp.expm1(2.0 * log_a)) * (i * u)
        if d == 1:
            a, bt = flip_parts(a, nc), flip_parts(bt, nc)
        hd = linear_scan(a, bt)
        h = h + (flip_parts(hd, nc) if d == 1 else hd)
    h = jnp.concatenate([h[:, :nc], from_colmajor(h[:, nc:])], axis=1)[:, r0:]
    return h * jax.nn.gelu(a_g[:, r0:], approximate=True)


def gla_branch(q, k, v, r, lr, nc, r0, wa2, ba, norm_g):
    bn, t = q.shape[:2]
    heads = lambda z, dh: z.reshape(bn, t, -1, dh).astype(jnp.float32)
    qh, kh, vh = heads(q, GLA_DK) * GLA_DK ** -0.5, heads(k, GLA_DK), heads(v, GLA_DV)
    lr = lr.reshape(bn, t, 2, GLA_RANK)
    o = 0.0
    for d in range(2):
        g = heads(jax.nn.log_sigmoid((lr[:, :, d] @ wa2[d] + ba[d]).astype(jnp.float32)) / GLA_TAU, GLA_DK)
        ins = (qh, kh, vh, g)
        if d == 1:
            ins = tuple(flip_parts(z, nc) for z in ins)
        od = gla_chunked(*ins)
        o = o + (flip_parts(od, nc) if d == 1 else od)
    return head_norm(o[:, r0:], norm_g) * jax.nn.silu(r[:, r0:])


def mlstm_branch(q, k, v, o, g, nc, r0, conv_w, conv_b, norm_g):
    bn, t = q.shape[:2]
    qk = jnp.concatenate([q, k], axis=-1)
    qk = jax.nn.silu(jnp.concatenate([centred_dwconv(qk[:, :nc], conv_w, conv_b),
                                      centred_dwconv(qk[:, nc:], conv_w, conv_b)], axis=1))
    heads = lambda z: z.reshape(bn, t, MLSTM_HEADS, MLSTM_D).astype(jnp.float32)
    qh, kh, vh = heads(qk[..., :MLSTM_W]), heads(qk[..., MLSTM_W:]) * MLSTM_D ** -0.5, heads(v)
    g = g.reshape(bn, t, 2, 2, MLSTM_HEADS).astype(jnp.float32)
    h = 0.0
    for d in range(2):
        ins = (qh, kh, vh, g[:, :, d, 0], jax.nn.log_sigmoid(g[:, :, d, 1]))
        if d == 1:
            ins = tuple(flip_parts(z, nc) for z in ins)
        hd = mlstm_chunked(*ins)
        h = h + (flip_parts(hd, nc) if d == 1 else hd)
    return head_norm(h[:, r0:], norm_g) * jax.nn.sigmoid(o[:, r0:])


def mixer(h_lat, h_ctx, w_in, b_in, conv_a_w, conv_a_b, lru_wa, lru_ba, lru_wx, lru_bx, lru_lam,
          gla_wa2, gla_ba, gla_norm_g, conv_c_w, conv_c_b, mlstm_norm_g, w_branch, w_out, with_ctx):
    nc = h_ctx.shape[1]
    h_all = jnp.concatenate([h_ctx, h_lat], axis=1)
    r0 = 0 if with_ctx else nc
    feat = h_all @ w_in[:, :N_FEAT] + b_in[:N_FEAT]
    a_x, a_g, b_q, b_k, b_v, b_r, b_lr, c_q, c_k, c_v, c_o, c_g = jnp.split(feat, IN_SPLIT, axis=-1)
    ys = (rglru_branch(a_x, a_g, nc, r0, conv_a_w, conv_a_b, lru_wa, lru_ba, lru_wx, lru_bx, lru_lam),
          gla_branch(b_q, b_k, b_v, b_r, b_lr, nc, r0, gla_wa2, gla_ba, gla_norm_g),
          mlstm_branch(c_q, c_k, c_v, c_o, c_g, nc, r0, conv_c_w, conv_c_b, mlstm_norm_g))
    h_sel = h_all[:, r0:]
    merged = 0.0
    for n, y in enumerate(ys):
        lo = N_FEAT + n * D_MODEL
        gate = jax.nn.sigmoid(h_sel @ w_in[:, lo:lo + D_MODEL] + b_in[lo:lo + D_MODEL])
        merged = merged + gate * (y.astype(h_all.dtype) @ w_branch[n])
    return merged @ w_out


def expert_choice_ffn(h, w_router, w_gate, w_up, w_down):
    bn, n, _ = h.shape
    cap = EC_CAPACITY * n // N_EXPERTS
    aff = jax.nn.softmax((h @ w_router).astype(jnp.float32), axis=-1)
    gv, idx = lax.top_k(jnp.swapaxes(aff, 1, 2), cap)
    bidx = jnp.arange(bn)[:, None, None]
    xg = h[bidx, idx]
    hid = jax.nn.silu(jnp.einsum('becd,edf->becf', xg, w_gate)) * jnp.einsum('becd,edf->becf', xg, w_up)
    ye = jnp.einsum('becf,efd->becd', hid, w_down) * gv[..., None].astype(h.dtype)
    return jnp.zeros_like(h).at[bidx, idx].add(ye)


def setup_inputs(seed: int = 0) -> dict:
    key = jax.random.key(seed)
    ks = jax.random.split(key, 32)
    nrm = lambda i, shape, scale: jax.random.normal(ks[i], shape, jnp.float32) * scale
    D, L, E = D_MODEL, DEPTH, N_EXPERTS
    beta = DEEPNORM_BETA
    bs = D_RNN // RNN_BLOCKS
    a8 = jax.random.uniform(ks[10], (L, 2, D_RNN), jnp.float32, 0.9, 0.999)
    a = a8 ** (1.0 / LRU_C)
    gate_b = jnp.tile(jnp.concatenate([jnp.zeros((MLSTM_HEADS,), jnp.float32), jnp.linspace(3.0, 6.0, MLSTM_HEADS)]), 2)
    b_in = nrm(6, (L, N_IN), 0.02).at[:, N_FEAT - 4 * MLSTM_HEADS:N_FEAT].add(gate_b)
    return {
        'x': nrm(0, (BATCH, SEQ, D), 1.0),
        'c': nrm(1, (BATCH, D), 1.0),
        'ctx': nrm(2, (BATCH, CTX_LEN, D), 1.0),
        'c_ctx': nrm(3, (D,), 1.0),
        'w_mod': nrm(4, (L, D, 6 * D), 0.5 * D ** -0.5),
        'b_mod': nrm(5, (L, 6 * D), 0.02),
        'w_in': nrm(7, (L, D, N_IN), D ** -0.5),
        'b_in': b_in,
        'conv_a_w': nrm(8, (L, CONV_W, D_RNN), CONV_W ** -0.5),
        'conv_a_b': nrm(9, (L, D_RNN), 0.02),
        'lru_wa': nrm(11, (L, 2, RNN_BLOCKS, bs, bs), bs ** -0.5),
        'lru_ba': nrm(12, (L, 2, D_RNN), 0.02),
        'lru_wx': nrm(13, (L, 2, RNN_BLOCKS, bs, bs), bs ** -0.5),
        'lru_bx': nrm(14, (L, 2, D_RNN), 0.02),
        'lru_lam': jnp.log(a) - jnp.log1p(-a),
        'gla_wa2': nrm(15, (L, 2, GLA_RANK, GLA_K), GLA_RANK ** -0.5),
        'gla_ba': nrm(16, (L, 2, GLA_K), 0.02),
        'gla_norm_g': 1.0 + nrm(17, (L, GLA_V), 0.02),
        'conv_c_w': nrm(18, (L, CONV_W, 2 * MLSTM_W), CONV_W ** -0.5),
        'conv_c_b': nrm(19, (L, 2 * MLSTM_W), 0.02),
        'mlstm_norm_g': 1.0 + nrm(20, (L, MLSTM_W), 0.02),
        'w_branch': nrm(21, (L, N_BRANCH, D_RNN, D), beta * D_RNN ** -0.5),
        'w_out': nrm(22, (L, D, D), beta * D ** -0.5),
        'ln1_g': 1.0 + nrm(23, (L, D), 0.02),
        'ln1_b': nrm(24, (L, D), 0.02),
        'w_router': nrm(25, (L, D, E), D ** -0.5),
        'w_e_gate': nrm(26, (L, E, D, MOE_FF), beta * D ** -0.5),
        'w_e_up': nrm(27, (L, E, D, MOE_FF), beta * D ** -0.5),
        'w_e_down': nrm(28, (L, E, MOE_FF, D), beta * MOE_FF ** -0.5),
        'ln2_g': 1.0 + nrm(29, (L, D), 0.02),
        'ln2_b': nrm(30, (L, D), 0.02),
    }


def reference(x, c, ctx, c_ctx, w_mod, b_mod, w_in, b_in, conv_a_w, conv_a_b, lru_wa, lru_ba, lru_wx, lru_bx,
              lru_lam, gla_wa2, gla_ba, gla_norm_g, conv_c_w, conv_c_b, mlstm_norm_g, w_branch, w_out, ln1_g, ln1_b,
              w_router, w_e_gate, w_e_up, w_e_down, ln2_g, ln2_b):
    nc = ctx.shape[1]
    for l in range(DEPTH):
        last = l == DEPTH - 1
        mod = jax.nn.silu(c) @ w_mod[l] + b_mod[l]
        mod_c = jax.nn.silu(c_ctx) @ w_mod[l] + b_mod[l]
        sh1, sc1, g1, sh2, sc2, g2 = jnp.split(mod[:, None, :], 6, axis=-1)
        sh1c, sc1c, g1c, sh2c, sc2c, g2c = jnp.split(mod_c, 6)
        y = mixer(x * (1 + sc1) + sh1, ctx * (1 + sc1c) + sh1c, w_in[l], b_in[l], conv_a_w[l], conv_a_b[l],
                  lru_wa[l], lru_ba[l], lru_wx[l], lru_bx[l], lru_lam[l], gla_wa2[l], gla_ba[l], gla_norm_g[l],
                  conv_c_w[l], conv_c_b[l], mlstm_norm_g[l], w_branch[l], w_out[l], not last)
        if last:
            y_lat = y
        else:
            y_lat = y[:, nc:]
            ctx = layer_norm(DEEPNORM_ALPHA * ctx + g1c * y[:, :nc], ln1_g[l], ln1_b[l])
        x = layer_norm(DEEPNORM_ALPHA * x + g1 * y_lat, ln1_g[l], ln1_b[l])
        f = expert_choice_ffn(x * (1 + sc2) + sh2, w_router[l], w_e_gate[l], w_e_up[l], w_e_down[l])
        x = layer_norm(DEEPNORM_ALPHA * x + g2 * f, ln2_g[l], ln2_b[l])
        if not last:
            fc = expert_choice_ffn(ctx * (1 + sc2c) + sh2c, w_router[l], w_e_gate[l], w_e_up[l], w_e_down[l])
            ctx = layer_norm(DEEPNORM_ALPHA * ctx + g2c * fc, ln2_g[l], ln2_b[l])
    return x
```

```python
from contextlib import ExitStack
import numpy as np
import concourse.bass as bass
import concourse.mybir as mybir
from concourse.bass_utils import run_bass_kernel_spmd

F32 = mybir.dt.float32
BF16 = mybir.dt.bfloat16
U32 = mybir.dt.uint32
AF = mybir.ActivationFunctionType
ALU = mybir.AluOpType
AX = mybir.AxisListType

D = 4096
NL = 4096
NCTX = 256
T = NL + NCTX
DEPTH = 2
N_FEAT = 9264
N_IN = 21552
NMOD = 6 * D
NCORES = 8


JOBS = []
_col = 0
for _name, _n, _o in (("a_x", 1024, "f"), ("a_g", 1024, "f"), ("b_q", 512, "f"), ("b_k", 512, "ft"), ("b_v", 1024, "t"),
                      ("b_r", 1024, "t"), ("b_lr0", 16, "f"), ("b_lr1", 16, "f"), ("c_q", 1024, "f"), ("c_k", 1024, "f"),
                      ("c_v", 1024, "t"), ("c_o", 1024, "t"), ("c_g", 16, "t")):
    JOBS.append((_name, _col, _n, _o))
    _col += _n
assert _col == N_FEAT
BF_COL = {}
NBF = 0
for _name, _c0, _n, _o in JOBS:
    if 'f' in _o:
        BF_COL[_name] = NBF
        NBF += (_n + 127) // 128


def host_bias_F(b_in_l):
    out = np.zeros((128, NBF), np.float32)
    for name, c0, n, o in JOBS:
        if 'f' in o:
            for j in range((n + 127) // 128):
                m = min(128, n - j * 128)
                out[:m, BF_COL[name] + j] = b_in_l[c0 + j * 128: c0 + j * 128 + m]
    return out


class DSem:
    def __init__(self, sem, did, nobarrier):
        self.sem, self.id, self.nobarrier, self.cnt = sem, did, nobarrier, 0


class Buf:
    def __init__(self, k, t, name, dma_target=False):
        self.k = k
        self.t = t
        self.name = name
        self.w = None
        self.r = {}
        self.ds = None

    def __getitem__(self, idx):
        return self.t[idx]


class K:
    def __init__(self, nc, es):
        self.nc = nc
        self.es = es
        self.eng = dict(pe=nc.tensor, dve=nc.vector, act=nc.scalar, pool=nc.gpsimd, sp=nc.sync)
        self.sem = {e: es.enter_context(nc.semaphore("sem_" + e)) for e in self.eng}
        self.cnt = {e: 0 for e in self.eng}
        self.seen = {e: {} for e in self.eng}
        self.dsems = []
        self.pool = []
        self.rr = 0
        self.nins = 0

    def sb(self, name, shape, dt, es=None):
        self.uid = getattr(self, 'uid', 0) + 1
        name = f"{name}_{self.uid}"
        t = (es or self.es).enter_context(self.nc.sbuf_tensor(name, list(shape), dt))
        return Buf(self, t, name)

    def ps(self, name, shape, dt, es=None):
        self.uid = getattr(self, 'uid', 0) + 1
        name = f"{name}_{self.uid}"
        t = (es or self.es).enter_context(self.nc.psum_tensor(name, list(shape), dt))
        return Buf(self, t, name)

    def dram(self, name, shape, dt, kind=None):
        if kind is None:
            t = self.nc.dram_tensor(name, list(shape), dt)
        else:
            t = self.nc.dram_tensor(name, list(shape), dt, kind=kind)
        return Buf(self, t, name)

    NPOOL = 70

    def new_dsem(self, name, nobarrier=False):
        ds = DSem(self.es.enter_context(self.nc.semaphore("ds_" + name)), len(self.dsems), nobarrier)
        self.dsems.append(ds)
        return ds

    def _dsem(self, b):
        if b.ds is None:
            if len(self.pool) < self.NPOOL:
                self.pool.append(self.new_dsem(f"p{len(self.pool)}"))
                b.ds = self.pool[-1]
            else:
                b.ds = self.pool[self.rr % self.NPOOL]
                self.rr += 1
        return b.ds

    def _waits(self, eng, reads, writes):
        need = {}

        def add(tok):
            if tok is None:
                return
            key = tok[:2]
            if key[0] == 'd':
                val = self.dsems[key[1]].cnt
            else:
                val = tok[2]
            if need.get(key, 0) < val:
                need[key] = val
        for b in reads:
            add(b.w)
        for b in writes:
            add(b.w)
            for tok in b.r.values():
                add(tok)
        e = self.eng[eng]
        for key, val in need.items():
            if key == ('e', 'pe') and eng == 'pe':
                continue
            if self.seen[eng].get(key, 0) >= val:
                continue
            if key[0] == 'd':
                e.wait_ge(self.dsems[key[1]].sem, val)
            else:
                e.wait_ge(self.sem[key[1]], val)
            self.seen[eng][key] = val

    def op(self, eng, reads, writes, emit):
        self._waits(eng, reads, writes)
        ins = emit(self.eng[eng])
        self.cnt[eng] += 1
        ins.then_inc(self.sem[eng], 1)
        tok = ('e', eng, self.cnt[eng])
        for b in reads:
            b.r[('e', eng)] = tok
        for b in writes:
            b.w = tok
            b.r = {}
        self.nins += 1
        return ins

    def dma(self, q, dst, dst_ap, src, src_ap, **kw):
        self._waits(q, [src], [dst])
        ds = self._dsem(dst)
        ins = self.eng[q].dma_start(out=dst_ap, in_=src_ap, **kw)
        ds.cnt += 16
        ins.then_inc(ds.sem, 16)
        tok = ('d', ds.id, ds.cnt)
        src.r[('d', ds.id)] = tok
        dst.w = tok
        dst.r = {}
        self.nins += 1

    def allgather(self, dst, src, groups):
        self._waits('pool', [src], [dst])
        ds = self._dsem(dst)
        ins = self.nc.gpsimd.collective_compute("AllGather", ALU.bypass, replica_groups=groups,
                                                ins=[src.t.ap().opt()], outs=[dst.t.ap().opt()])
        ds.cnt += 1
        ins.then_inc(ds.sem, 1)
        tok = ('d', ds.id, ds.cnt)
        src.r[('d', ds.id)] = tok
        dst.w = tok
        dst.r = {}

    def wait_all(self, eng, bufs):
        self._waits(eng, bufs, [])

    def barrier(self):
        for eng, e in self.eng.items():
            if eng == 'pool':
                continue
            for o in self.eng:
                if o == eng or self.cnt[o] == 0:
                    continue
                key = ('e', o)
                if self.seen[eng].get(key, 0) < self.cnt[o]:
                    e.wait_ge(self.sem[o], self.cnt[o])
                    self.seen[eng][key] = self.cnt[o]
            for ds in self.dsems:
                if ds.nobarrier or ds.cnt == 0:
                    continue
                key = ('d', ds.id)
                if self.seen[eng].get(key, 0) < ds.cnt:
                    e.wait_ge(ds.sem, ds.cnt)
                    self.seen[eng][key] = ds.cnt

    def mm(self, ps, ps_ap, lhsT, lhsT_ap, rhs, rhs_ap, start, stop):
        if stop:
            self.op('pe', [lhsT, rhs], [ps], lambda e: e.matmul(ps_ap, lhsT_ap, rhs_ap, start=start, stop=stop))
            return
        self._waits('pe', [lhsT, rhs], [ps])
        self.eng['pe'].matmul(ps_ap, lhsT_ap, rhs_ap, start=start, stop=stop)
        tok = ('e', 'pe', self.cnt['pe'] + 1)
        lhsT.r[('e', 'pe')] = tok
        rhs.r[('e', 'pe')] = tok
        ps.w = tok
        ps.r = {}
        self.nins += 1

    def tr(self, ps, ps_ap, src, src_ap, ident, ident_ap):
        self.op('pe', [src, ident], [ps], lambda e: e.transpose(ps_ap, src_ap, ident_ap))

    def act(self, dst, dst_ap, src, src_ap, func, bias=None, scale=None, extra_reads=(), accum=None, accum_ap=None):
        kw = {}
        if bias is not None:
            kw['bias'] = bias
        if scale is not None:
            kw['scale'] = scale
        wr = [dst]
        if accum is not None:
            kw['accum_out'] = accum_ap
            wr.append(accum)
        self.op('act', [src] + list(extra_reads), wr, lambda e: e.activation(dst_ap, src_ap, func, **kw))

    def tt(self, dst, dst_ap, a, a_ap, b, b_ap, op, eng='dve'):
        self.op(eng, [a, b], [dst], lambda e: e.tensor_tensor(dst_ap, a_ap, b_ap, op))

    def ts(self, dst, dst_ap, a, a_ap, s1, s2, op0, op1=None, extra_reads=(), eng='dve'):
        if op1 is None:
            self.op(eng, [a] + list(extra_reads), [dst], lambda e: e.tensor_scalar(dst_ap, a_ap, s1, None, op0))
        else:
            self.op(eng, [a] + list(extra_reads), [dst], lambda e: e.tensor_scalar(dst_ap, a_ap, s1, s2, op0, op1))

    def stt(self, dst, dst_ap, a, a_ap, scalar, b, b_ap, op0, op1, extra_reads=()):
        self.op('dve', [a, b] + list(extra_reads), [dst],
                lambda e: e.scalar_tensor_tensor(dst_ap, a_ap, scalar, b_ap, op0, op1))

    def copy(self, dst, dst_ap, src, src_ap, eng='dve'):
        if eng == 'act':
            self.op('act', [src], [dst], lambda e: e.copy(dst_ap, src_ap))
        else:
            self.op(eng, [src], [dst], lambda e: e.tensor_copy(dst_ap, src_ap))

    def memset(self, dst, dst_ap, val, eng='dve'):
        self.op(eng, [], [dst], lambda e: e.memset(dst_ap, val))


class Gathered:
    def __init__(self, pieces, RS, PR, C):
        self.pieces, self.RS, self.PR, self.C = pieces, RS, PR, C

    def rows(self, g0, n):
        r, rem = divmod(g0, self.RS)
        q, i0 = divmod(rem, self.PR)
        assert i0 + n <= self.PR, (g0, n, self.RS, self.PR)
        b = self.pieces[q]
        return b, b.t[r * self.PR + i0: r * self.PR + i0 + n, :]


def gather_weight(k, name, shard, R_shard, C, out_dt, nsplit, cast):
    PR = R_shard // nsplit
    pieces = []
    for q in range(nsplit):
        bounce = k.dram(f"{name}_b{q}", [PR, C], out_dt)
        full = k.dram(f"{name}_g{q}", [NCORES * PR, C], out_dt)
        if not hasattr(k, 'bounce_ds'):
            k.bounce_ds = k.new_dsem("bounce", nobarrier=True)
        if q == 0:
            gds = k.new_dsem("g_" + name, nobarrier=True)
        bounce.ds = k.bounce_ds
        full.ds = gds
        RB = 2048
        for r0 in range(0, PR, RB):
            r1 = min(PR, r0 + RB)
            for c0 in range(0, C, 2048):
                c1 = min(C, c0 + 2048)
                k.dma('pool' if cast else 'sp', bounce, bounce.t[r0:r1, c0:c1], shard,
                      shard.t[q * PR + r0: q * PR + r1, c0:c1])
        k.allgather(full, bounce, [list(range(NCORES))])
        pieces.append(full)
    return Gathered(pieces, R_shard, PR, C)


def build_program(stages, dbg):
    nc = bass.Bass("TRN2", target_bir_lowering=False)
    es = ExitStack()
    k = K(nc, es)
    ein = lambda name, shape, dt=F32: k.dram(name, shape, dt, kind="ExternalInput")
    outs = {}

    NLAY = DEPTH if 'l1' in stages else 1
    X0 = ein("xin", [T, D])
    cT = ein("cT", [128, 32, 2])
    w_mod_s = [ein(f"w_mod{l}", [512, NMOD]) for l in range(NLAY)]
    b_mod = [ein(f"b_mod{l}", [1, NMOD]) for l in range(NLAY)]
    w_in_s = [ein(f"w_in{l}", [512, N_IN]) for l in range(NLAY)]
    b_in = [ein(f"b_in{l}", [1, N_IN]) for l in range(NLAY)]
    b_inF = [ein(f"b_inF{l}", [128, NBF]) for l in range(NLAY)]

    ident_bf = k.sb("ident_bf", [128, 128], BF16)
    ident_f = k.sb("ident_f", [128, 128], F32)
    identin = ein("ident", [128, 128])
    k.dma('sp', ident_f, ident_f[:], identin, identin.t[:, :])
    k.copy(ident_bf, ident_bf[:], ident_f, ident_f[:])
    PS = [k.ps(f"ps{i}", [128, 512], F32) for i in range(8)]

    Wmod, Win, Wbr, Wout = [], [], [], []
    GATHER_LATER = []
    for l in range(NLAY):
        Wmod.append(gather_weight(k, f"wmod{l}", w_mod_s[l], 512, NMOD, F32, 2, cast=False))
        Win.append(gather_weight(k, f"win{l}", w_in_s[l], 512, N_IN, BF16, 1, cast=True))
        GATHER_LATER.append(l)

    MODP = [k.dram(f"modp{l}", [2, NMOD], F32) for l in range(NLAY)]

    def stage_mod(l):
        with ExitStack() as s:
            cs = k.sb("cs", [128, 32, 2], F32, s)
            k.dma('sp', cs, cs[:], cT, cT.t[:, :, :])
            sg = k.sb("csg", [128, 32, 2], F32, s)
            k.act(sg, sg[:], cs, cs[:], AF.Sigmoid)
            k.tt(cs, cs[:], cs, cs[:], sg, sg[:], ALU.mult)
            wb = [k.sb(f"wmodsl{i}", [128, 16, 512], F32, s) for i in range(2)]
            bb = k.sb("bmodsb", [2, 512], F32, s)
            ob = k.sb("modout", [2, 512], F32, s)
            nld = 0
            for nb in range(NMOD // 512):
                ps = PS[nb % 2]
                for half in range(2):
                    w = wb[nld % 2]
                    nld += 1
                    for kk in range(16):
                        kc = half * 16 + kk
                        sb_, ap = Wmod[l].rows(kc * 128, 128)
                        k.dma('sp' if kk % 2 == 0 else 'act', w, w[:, kk, :], sb_, ap[:, nb * 512:(nb + 1) * 512])
                    for kk in range(16):
                        kc = half * 16 + kk
                        k.mm(ps, ps[0:2, :], cs, cs[:, kc, :], w, w[:, kk, :], start=(kc == 0), stop=(kc == 31))
                k.dma('sp', bb, bb[:], b_mod[l], b_mod[l].t[0, nb * 512:(nb + 1) * 512].partition_broadcast(2))
                k.tt(ob, ob[:], ps, ps[0:2, :], bb, bb[:], ALU.add)
                if (nb * 512) // D in (1, 4):
                    k.ts(ob, ob[:], ob, ob[:], 1.0, None, ALU.add)
                k.dma('sp', MODP[l], MODP[l].t[:, nb * 512:(nb + 1) * 512], ob, ob[:])

    NBLK = 9
    HT = [k.dram(f"ht{b}", [128, 32, 512], BF16) for b in range(NBLK)]

    def stage_A(l, X, which, Hout=None):
        with ExitStack() as s:
            SC = [k.sb(f"SC{m}", [128, D], F32, s) for m in range(2)]
            SH = [k.sb(f"SH{m}", [128, D], F32, s) for m in range(2)]
            for m in range(2):
                k.dma('sp', SH[m], SH[m][:], MODP[l], MODP[l].t[m, which * D:(which + 1) * D].partition_broadcast(128))
                k.dma('act', SC[m], SC[m][:], MODP[l], MODP[l].t[m, (which + 1) * D:(which + 2) * D].partition_broadcast(128))
            xt = [k.sb(f"xt{i}", [128, D], F32, s) for i in range(2)]
            hb = [k.sb(f"hb{i}", [128, D], BF16, s) for i in range(2)]
            ho = [k.sb(f"ho{i}", [128, 32, 128], BF16, s) for i in range(2)]
            PB = [Buf(k, PS[i].t[:].bitcast(BF16), f"psbf{i}") for i in range(4)]
            for i in range(T // 128):
                m = 0 if i < NL // 128 else 1
                x_, h_, o_ = xt[i % 2], hb[i % 2], ho[i % 2]
                k.dma('sp' if i % 2 == 0 else 'act', x_, x_[:], X, X.t[i * 128:(i + 1) * 128, :])
                k.tt(x_, x_[:], x_, x_[:], SC[m], SC[m][:], ALU.mult)
                k.tt(h_, h_[:], x_, x_[:], SH[m], SH[m][:], ALU.add)
                if Hout is not None:
                    k.dma('act', Hout, Hout.t[i * 128:(i + 1) * 128, :], h_, h_[:])
                for g in range(4):
                    pb = PS[g]
                    pbv = pb.t[:].bitcast(BF16)
                    for j in range(8):
                        kc = g * 8 + j
                        k.tr(pb, pbv[:, j * 128:(j + 1) * 128], h_, h_[:, kc * 128:(kc + 1) * 128], ident_bf, ident_bf[:])
                    dst_ap = o_[:, g * 8:(g + 1) * 8, :]
                    src_ap = pbv.rearrange("p (j t) -> p j t", j=8)
                    if g % 2 == 0:
                        k.copy(o_, dst_ap, pb, src_ap, eng='act')
                    else:
                        k.copy(o_, dst_ap, pb, src_ap, eng='dve')
                blk, off = divmod(i, 4)
                k.dma('sp', HT[blk], HT[blk].t[:, :, off * 128:(off + 1) * 128], o_, o_[:])

    FT, TM = {}, {}
    jobs = JOBS
    for name, c0, n, orient in jobs:
        if 'f' in orient:
            FT[name] = k.dram("FT_" + name, [n, T], F32)
        if 't' in orient:
            TM[name] = k.dram("TM_" + name, [T, n], F32)

    def stage_B(l):
        with ExitStack() as s:
            hts = [k.sb(f"hts{i}", [128, 32, 512], BF16, s) for i in range(2)]
            wsl = [k.sb(f"wsl{i}", [128, 32, 512], BF16, s) for i in range(2)]
            bT = k.sb("bT", [128, NBF], F32, s)
            k.dma('sp', bT, bT[:], b_inF[l], b_inF[l].t[:, :])
            bbc = [k.sb(f"bbc{i}", [128, 512], F32, s) for i in range(2)]
            osb = [k.sb(f"osb{i}", [128, 512], F32, s) for i in range(4)]
            nw = 0
            no = 0
            wfull = Win[l].pieces[0]
            for blk in range(NBLK):
                N = 512 if blk < 8 else 256
                t0 = blk * 512
                h = hts[blk % 2]
                k.dma('sp', h, h[:, :, 0:N], HT[blk], HT[blk].t[:, :, 0:N])
                for name, c0, n, orient in jobs:
                    for s0 in range(0, n, 512):
                        sn = min(512, n - s0)
                        w = wsl[nw % 2]
                        bc = bbc[nw % 2]
                        nw += 1
                        cc = c0 + s0
                        k.dma('act', w, w[:, :, 0:sn], wfull,
                              wfull.t[:, cc:cc + sn].rearrange("(k p) c -> p k c", p=128))
                        if 't' in orient:
                            k.dma('sp', bc, bc[:, 0:sn], b_in[l], b_in[l].t[0, cc:cc + sn].partition_broadcast(128))
                        if 'f' in orient:
                            for m0 in range(0, sn, 128):
                                mn = min(128, sn - m0)
                                ps = PS[no % 8]
                                o = osb[no % 4]
                                no += 1
                                for kc in range(32):
                                    k.mm(ps, ps[0:mn, 0:N], w, w[:, kc, m0:m0 + mn], h, h[:, kc, 0:N],
                                         start=(kc == 0), stop=(kc == 31))
                                bj = BF_COL[name] + (s0 + m0) // 128
                                bias_ap = bT[0:mn, bj:bj + 1]
                                k.act(o, o[0:mn, 0:N], ps, ps[0:mn, 0:N], AF.Identity, bias=bias_ap, extra_reads=[bT])
                                dst = FT[name]
                                k.dma('sp', dst, dst.t[s0 + m0:s0 + m0 + mn, t0:t0 + N], o, o[0:mn, 0:N])
                        if 't' in orient:
                            for tt in range(N // 128):
                                ps = PS[no % 8]
                                o = osb[no % 4]
                                no += 1
                                for kc in range(32):
                                    k.mm(ps, ps[:, 0:sn], h, h[:, kc, tt * 128:(tt + 1) * 128], w, w[:, kc, 0:sn],
                                         start=(kc == 0), stop=(kc == 31))
                                k.tt(o, o[:, 0:sn], ps, ps[:, 0:sn], bc, bc[:, 0:sn], ALU.add)
                                dst = TM[name]
                                k.dma('sp', dst, dst.t[t0 + tt * 128:t0 + (tt + 1) * 128, s0:s0 + sn], o, o[:, 0:sn])

    lru_par = [ein(f"lru_par{l}", [128, 8, 12]) for l in range(NLAY)]
    lru_w = [ein(f"lru_w{l}", [128, 8, 4, 128]) for l in range(NLAY)]
    YT = k.dram("YT", [3, 8, 128, T], BF16)
    BW = 4360
    LAT0, CTX0 = 1, 4100

    def stage_R(l):
        with ExitStack() as s:
            par = k.sb("lrupar", [128, 8, 12], F32, s)
            W = k.sb("lruw", [128, 8, 4, 128], F32, s)
            k.dma('sp', par, par[:], lru_par[l], lru_par[l].t[:, :, :])
            k.dma('act', W, W[:], lru_w[l], lru_w[l].t[:, :, :, :])
            e_ = k.sb("lru_e", [128, 8, 2], F32, s)
            t_ = k.sb("lru_t", [128, 8, 2], F32, s)
            yb = k.sb("lru_yb", [128, 8, 2], F32, s)
            m_ = k.sb("lru_m", [128, 8, 2], F32, s)
            c1 = k.sb("lru_c1", [128, 8, 2], F32, s)
            c2 = k.sb("lru_c2", [128, 8, 2], F32, s)
            k.act(e_, e_[:], par, par[:, :, 9:11], AF.Exp, scale=-1.0)
            k.ts(t_, t_[:], e_, e_[:], 0.2, -0.25, ALU.mult, ALU.add)
            k.tt(t_, t_[:], t_, t_[:], e_, e_[:], ALU.mult)
            k.ts(t_, t_[:], t_, t_[:], 1.0 / 3.0, None, ALU.add)
            k.tt(t_, t_[:], t_, t_[:], e_, e_[:], ALU.mult)
            k.ts(t_, t_[:], t_, t_[:], -0.5, None, ALU.add)
            k.tt(t_, t_[:], t_, t_[:], e_, e_[:], ALU.mult)
            k.ts(t_, t_[:], t_, t_[:], 1.0, None, ALU.add)
            k.tt(t_, t_[:], t_, t_[:], e_, e_[:], ALU.mult)
            k.act(yb, yb[:], e_, e_[:], AF.Ln, bias=1.0)
            k.ts(m_, m_[:], e_, e_[:], 0.1, None, ALU.is_lt)
            k.tt(t_, t_[:], t_, t_[:], yb, yb[:], ALU.subtract)
            k.tt(t_, t_[:], t_, t_[:], m_, m_[:], ALU.mult)
            k.tt(t_, t_[:], t_, t_[:], yb, yb[:], ALU.add)
            k.ts(c1, c1[:], t_, t_[:], -8.0, None, ALU.mult)
            k.ts(c2, c2[:], t_, t_[:], -16.0, None, ALU.mult)

            abuf = k.sb("lru_abuf", [128, BW], F32, s)
            u = k.sb("lru_u", [128, BW], F32, s)
            a_ = k.sb("lru_a", [128, BW], F32, s)
            bt = k.sb("lru_bt", [128, BW], F32, s)
            h0 = k.sb("lru_h0", [128, BW], F32, s)
            h1 = k.sb("lru_h1", [128, BW], F32, s)
            tmp = k.sb("lru_tmp", [128, T], F32, s)
            g_ = k.sb("lru_g", [128, T], F32, s)
            yo = k.sb("lru_yo", [128, T], BF16, s)
            ch = [k.sb(f"lru_ch{i}", [128, 512], F32, s) for i in range(4)]
            npz = 0
            for n in range(8):
                k.memset(abuf, abuf[:], 0.0)
                k.dma('sp', tmp, tmp[:, 0:NL], FT["a_x"], FT["a_x"].t[n * 128:(n + 1) * 128, 0:NL])
                k.dma('act', abuf, abuf[:, CTX0:CTX0 + NCTX], FT["a_x"], FT["a_x"].t[n * 128:(n + 1) * 128, NL:T])
                k.copy(abuf, abuf[:, LAT0:LAT0 + NL].rearrange("p (c r) -> p c r", c=64),
                       tmp, tmp[:, 0:NL].rearrange("p (r c) -> p c r", r=64))
                lo, hi = 1, BW - 3
                k.ts(u, u[:, lo:hi], abuf, abuf[:, lo - 1:hi - 1], par[:, n, 0:1], par[:, n, 4:5], ALU.mult, ALU.add,
                     extra_reads=[par])
                for kk in range(1, 4):
                    k.stt(u, u[:, lo:hi], abuf, abuf[:, lo - 1 + kk:hi - 1 + kk], par[:, n, kk:kk + 1], u, u[:, lo:hi],
                          ALU.mult, ALU.add, extra_reads=[par])
                for d in range(2):
                    for j0 in range(0, BW, 512):
                        w = min(512, BW - j0)
                        if j0 == 0:
                            j0, w = 1, 511
                        if j0 + w > BW - 3:
                            w = BW - 3 - j0
                        pa, px = PS[npz % 8], PS[(npz + 1) % 8]
                        npz += 2
                        k.mm(pa, pa[:, 0:w], W, W[:, n, d * 2 + 0, :], u, u[:, j0:j0 + w], True, True)
                        k.mm(px, px[:, 0:w], W, W[:, n, d * 2 + 1, :], u, u[:, j0:j0 + w], True, True)
                        rt, it, a2, lg = ch
                        k.act(rt, rt[:, 0:w], pa, pa[:, 0:w], AF.Sigmoid, bias=par[:, n, 5 + 2 * d:6 + 2 * d], extra_reads=[par])
                        k.act(it, it[:, 0:w], px, px[:, 0:w], AF.Sigmoid, bias=par[:, n, 6 + 2 * d:7 + 2 * d], extra_reads=[par])
                        k.act(a_, a_[:, j0:j0 + w], rt, rt[:, 0:w], AF.Exp, scale=c1[:, n, d:d + 1], extra_reads=[c1])
                        k.act(a2, a2[:, 0:w], rt, rt[:, 0:w], AF.Exp, scale=c2[:, n, d:d + 1], extra_reads=[c2])
                        k.act(lg, lg[:, 0:w], a2, a2[:, 0:w], AF.Ln, bias=1.0000001, scale=-1.0)
                        k.act(lg, lg[:, 0:w], lg, lg[:, 0:w], AF.Exp, scale=0.5)
                        k.tt(it, it[:, 0:w], it, it[:, 0:w], lg, lg[:, 0:w], ALU.mult)
                        k.tt(bt, bt[:, j0:j0 + w], it, it[:, 0:w], u, u[:, j0:j0 + w], ALU.mult)
                    hd = h0 if d == 0 else h1
                    cs_, ls_ = slice(CTX0, CTX0 + NCTX), slice(LAT0, LAT0 + NL)
                    if d == 0:
                        k.op('dve', [a_, bt], [hd], lambda e: e.tensor_tensor_scan(
                            hd[:, cs_], a_[:, cs_], bt[:, cs_], 0.0, ALU.mult, ALU.add))
                        k.op('dve', [a_, bt, hd], [hd], lambda e: e.tensor_tensor_scan(
                            hd[:, ls_], a_[:, ls_], bt[:, ls_], hd[:, CTX0 + NCTX - 1:CTX0 + NCTX], ALU.mult, ALU.add))
                    else:
                        k.op('dve', [a_, bt], [hd], lambda e: e.tensor_tensor_scan(
                            hd[:, cs_][:, ::-1], a_[:, cs_][:, ::-1], bt[:, cs_][:, ::-1], 0.0, ALU.mult, ALU.add))
                        k.op('dve', [a_, bt, hd], [hd], lambda e: e.tensor_tensor_scan(
                            hd[:, ls_][:, ::-1], a_[:, ls_][:, ::-1], bt[:, ls_][:, ::-1], hd[:, CTX0:CTX0 + 1],
                            ALU.mult, ALU.add))
                k.tt(h0, h0[:, 1:BW - 3], h0, h0[:, 1:BW - 3], h1, h1[:, 1:BW - 3], ALU.add)
                k.dma('sp', tmp, tmp[:], FT["a_g"], FT["a_g"].t[n * 128:(n + 1) * 128, :])
                k.tt(g_, g_[:], tmp, tmp[:], tmp, tmp[:], ALU.mult)
                k.ts(g_, g_[:], g_, g_[:], 0.044715, 1.0, ALU.mult, ALU.add)
                k.tt(g_, g_[:], g_, g_[:], tmp, tmp[:], ALU.mult)
                k.act(g_, g_[:], g_, g_[:], AF.Sigmoid, scale=1.5957691216057308)
                k.tt(g_, g_[:], g_, g_[:], tmp, tmp[:], ALU.mult)
                k.tt(yo, yo[:, 0:NL].rearrange("p (r c) -> p r c", r=64),
                     h0, h0[:, LAT0:LAT0 + NL].rearrange("p (c r) -> p r c", c=64),
                     g_, g_[:, 0:NL].rearrange("p (r c) -> p r c", r=64), ALU.mult)
                k.tt(yo, yo[:, NL:T], h0, h0[:, CTX0:CTX0 + NCTX], g_, g_[:, NL:T], ALU.mult)
                k.dma('sp', YT, YT.t[0, n, :, :], yo, yo[:])

    gla_c = ein("gla_c", [64, 6, 64])
    gla_wa2 = [ein(f"gla_wa2_{l}", [16, 2, 512]) for l in range(NLAY)]
    gla_ba = [ein(f"gla_ba_{l}", [1, 2, 512]) for l in range(NLAY)]
    gla_ng = [ein(f"gla_ng_{l}", [1, 1024]) for l in range(NLAY)]
    OG = k.dram("OG", [4, T, 256], F32)
    CH_F = list(range(64, 68)) + list(range(64))
    CH_B = list(range(67, 63, -1)) + list(range(63, -1, -1))
    EPS = 1e-5

    def stage_G(l):
        with ExitStack() as s:
            C = k.sb("gla_C", [64, 6, 64], F32, s)
            k.dma('sp', C, C[:], gla_c, gla_c.t[:, :, :])
            wa2 = k.sb("gla_wa2", [16, 2, 512], F32, s)
            k.dma('sp', wa2, wa2[:], gla_wa2[l], gla_wa2[l].t[:, :, :])
            ba = k.sb("gla_ba", [1, 2, 512], F32, s)
            k.dma('sp', ba, ba[:], gla_ba[l], gla_ba[l].t[:, :, :])
            ng = k.sb("gla_ng", [64, 1024], F32, s)
            k.dma('sp', ng, ng[:], gla_ng[l], gla_ng[l].t[0, :].partition_broadcast(64))
            ones = k.sb("gla_ones", [1, 64], F32, s)
            k.memset(ones, ones[:], 1.0)
            lrT = [k.sb(f"gla_lr{d}", [16, T], F32, s) for d in range(2)]
            for d in range(2):
                k.dma('sp', lrT[d], lrT[d][:], FT[f"b_lr{d}"], FT[f"b_lr{d}"].t[:, :])
            qT = k.sb("gla_qT", [128, T], F32, s)
            kT = k.sb("gla_kT", [128, T], F32, s)
            yT = k.sb("gla_yT", [128, 2, T], BF16, s)
            S = k.sb("gla_S", [128, 256], F32, s)
            vch = [k.sb(f"gla_v{i}", [64, 256], F32, s) for i in range(2)]
            kch = [k.sb(f"gla_k{i}", [64, 128], F32, s) for i in range(2)]
            rch = [k.sb(f"gla_r{i}", [64, 256], F32, s) for i in range(2)]
            ofc = [k.sb(f"gla_of{i}", [64, 256], F32, s) for i in range(2)]
            e_ = k.sb("gla_e", [64, 128], F32, s)
            gp = k.sb("gla_gp", [64, 128], F32, s)
            bTs = k.sb("gla_bTs", [128, 64], F32, s)
            nbm = k.sb("gla_nbm", [128, 1], F32, s)
            E1 = k.sb("gla_E1", [128, 64], F32, s)
            E2 = k.sb("gla_E2", [128, 64], F32, s)
            Eb = k.sb("gla_Eb", [128, 64], F32, s)
            dec = k.sb("gla_dec", [128, 1], F32, s)
            Ek = k.sb("gla_Ek", [64, 128], F32, s)
            qd = k.sb("gla_qd", [128, 64], F32, s)
            kd = k.sb("gla_kd", [128, 64], F32, s)
            qb = k.sb("gla_qb", [128, 64], F32, s)
            kb = k.sb("gla_kb", [64, 128], F32, s)
            attm = k.sb("gla_attm", [64, 64], F32, s)
            osb = k.sb("gla_osb", [64, 256], F32, s)
            st = k.sb("gla_st", [64, 6], F32, s)
            mv = k.sb("gla_mv", [64, 2], F32, s)
            rstd = k.sb("gla_rstd", [64, 1], F32, s)
            sg = k.sb("gla_sg", [64, 256], F32, s)
            yts = k.sb("gla_yts", [64, 256], F32, s)
            Pz, PbT, Prb, Patt, Po, PU, Ptr = PS[0], PS[1], PS[2], PS[3], PS[4], PS[5], PS[6]
            SC_Q = 128 ** -0.5
            it = 0
            for h in range(4):
                k.dma('sp', qT, qT[:], FT["b_q"], FT["b_q"].t[h * 128:(h + 1) * 128, :])
                k.dma('act', kT, kT[:], FT["b_k"], FT["b_k"].t[h * 128:(h + 1) * 128, :])
                for d in range(2):
                    k.memset(S, S[:], 0.0)
                    ci_mask, ci_tri, ci_sut = (0, 1, 2) if d == 0 else (3, 4, 5)
                    mid, last = (31, 63) if d == 0 else (32, 0)
                    for c in (CH_F if d == 0 else CH_B):
                        t0 = c * 64
                        tsl = slice(t0, t0 + 64)
                        v_, k_, r_, of_ = vch[it % 2], kch[it % 2], rch[it % 2], ofc[it % 2]
                        it += 1
                        k.dma('sp', v_, v_[:], TM["b_v"], TM["b_v"].t[tsl, h * 256:(h + 1) * 256])
                        k.dma('act', k_, k_[:], TM["b_k"], TM["b_k"].t[tsl, h * 128:(h + 1) * 128])
                        if d == 1:
                            k.dma('sp', r_, r_[:], TM["b_r"], TM["b_r"].t[tsl, h * 256:(h + 1) * 256])
                            k.dma('act', of_, of_[:], OG, OG.t[h, tsl, :])
                        k.mm(Pz, Pz[0:64, 0:128], lrT[d], lrT[d][:, tsl], wa2, wa2[:, d, h * 128:(h + 1) * 128], True, False)
                        k.mm(Pz, Pz[0:64, 0:128], ones, ones[:], ba, ba[:, d, h * 128:(h + 1) * 128], False, True)
                        k.act(e_, e_[:], Pz, Pz[0:64, 0:128], AF.Exp, scale=-1.0)
                        k.act(gp, gp[:], e_, e_[:], AF.Ln, bias=1.0)
                        k.mm(PbT, PbT[:, 0:64], gp, gp[:], C, C[:, ci_tri, :], True, True)
                        k.mm(Prb, Prb[0:64, 0:128], C, C[:, ci_sut, :], gp, gp[:], True, True)
                        k.copy(bTs, bTs[:], PbT, PbT[:, 0:64], eng='act')
                        k.ts(nbm, nbm[:], bTs, bTs[:, mid:mid + 1], -1.0, None, ALU.mult)
                        k.act(E1, E1[:], bTs, bTs[:], AF.Exp, bias=nbm[:, 0:1], extra_reads=[nbm])
                        k.act(E2, E2[:], bTs, bTs[:], AF.Exp, bias=bTs[:, mid:mid + 1], scale=-1.0)
                        k.act(Eb, Eb[:], bTs, bTs[:], AF.Exp)
                        k.act(dec, dec[:], bTs, bTs[:, last:last + 1], AF.Exp)
                        k.act(Ek, Ek[:], Prb, Prb[0:64, 0:128], AF.Exp)
                        k.stt(qd, qd[:], qT, qT[:, tsl], SC_Q, E1, E1[:], ALU.mult, ALU.mult)
                        k.tt(kd, kd[:], kT, kT[:, tsl], E2, E2[:], ALU.mult)
                        k.stt(qb, qb[:], qT, qT[:, tsl], SC_Q, Eb, Eb[:], ALU.mult, ALU.mult)
                        k.tt(kb, kb[:], k_, k_[:], Ek, Ek[:], ALU.mult)
                        k.mm(Patt, Patt[0:64, 0:64], kd, kd[:], qd, qd[:], True, True)
                        k.tt(attm, attm[:], Patt, Patt[0:64, 0:64], C, C[:, ci_mask, :], ALU.mult)
                        k.mm(Po, Po[0:64, 0:256], attm, attm[:], v_, v_[:], True, False)
                        k.mm(Po, Po[0:64, 0:256], qb, qb[:], S, S[:], False, True)
                        k.mm(PU, PU[:, 0:256], kb, kb[:], v_, v_[:], True, True)
                        k.stt(S, S[:], S, S[:], dec[:, 0:1], PU, PU[:, 0:256], ALU.mult, ALU.add, extra_reads=[dec])
                        if d == 0:
                            k.copy(osb, osb[:], Po, Po[0:64, 0:256], eng='act')
                            k.dma('sp', OG, OG.t[h, tsl, :], osb, osb[:])
                        else:
                            k.tt(osb, osb[:], Po, Po[0:64, 0:256], of_, of_[:], ALU.add)
                            k.op('dve', [osb], [st], lambda e: e.bn_stats(st[:], osb[:]))
                            k.op('dve', [st], [mv], lambda e: e.bn_aggr(mv[:], st[:]))
                            k.act(rstd, rstd[:], mv, mv[:, 1:2], AF.Ln, bias=EPS)
                            k.act(rstd, rstd[:], rstd, rstd[:], AF.Exp, scale=-0.5)
                            k.ts(osb, osb[:], osb, osb[:], mv[:, 0:1], rstd[:, 0:1], ALU.subtract, ALU.mult, extra_reads=[mv, rstd])
                            k.tt(osb, osb[:], osb, osb[:], ng, ng[:, h * 256:(h + 1) * 256], ALU.mult)
                            k.act(sg, sg[:], r_, r_[:], AF.Sigmoid)
                            k.tt(sg, sg[:], sg, sg[:], r_, r_[:], ALU.mult)
                            k.tt(yts, yts[:], osb, osb[:], sg, sg[:], ALU.mult)
                            for hf in range(2):
                                k.tr(Ptr, Ptr[:, hf * 64:(hf + 1) * 64], yts, yts[:, hf * 128:(hf + 1) * 128], ident_f, ident_f[0:64, 0:64])
                            k.copy(yT, yT[:, :, tsl], Ptr, Ptr[:, 0:128].rearrange("p (a t) -> p a t", a=2), eng='act')
                for hf in range(2):
                    k.dma('sp', YT, YT.t[1, h * 2 + hf, :, :], yT, yT[:, hf, :])

    ml_c = ein("ml_c", [64, 6, 64])
    ml_c2 = ein("ml_c2", [64, 2, 128])
    mlc_par = [ein(f"mlc_par{l}", [128, 16, 5]) for l in range(NLAY)]
    ml_ng = [ein(f"ml_ng{l}", [1, 1024]) for l in range(NLAY)]
    OM = k.dram("OM", [2, T, 256], F32)
    KTM = k.dram("KTM", [T, 256], F32)

    def stage_M(l):
        with ExitStack() as s:
            MC = k.sb("ml_C", [64, 6, 64], F32, s)
            MC2 = k.sb("ml_C2", [64, 2, 128], F32, s)
            k.dma('sp', MC, MC[:], ml_c, ml_c.t[:, :, :])
            k.dma('sp', MC2, MC2[:], ml_c2, ml_c2.t[:, :, :])
            cpar = k.sb("ml_cpar", [128, 16, 5], F32, s)
            k.dma('sp', cpar, cpar[:], mlc_par[l], mlc_par[l].t[:, :, :])
            ng = k.sb("ml_ng", [64, 1024], F32, s)
            k.dma('sp', ng, ng[:], ml_ng[l], ml_ng[l].t[0, :].partition_broadcast(64))
            G = k.sb("ml_G", [64, 68, 16], F32, s)
            k.dma('sp', G, G[:], TM["c_g"], TM["c_g"].t[:, :].rearrange("(c p) g -> p c g", p=64))
            LFP = k.sb("ml_LFP", [64, 68, 8], F32, s)
            for d in range(2):
                k.act(LFP, LFP[:, :, d * 4:(d + 1) * 4], G, G[:, :, d * 8 + 4:d * 8 + 8], AF.Exp, scale=-1.0)
            k.act(LFP, LFP[:], LFP, LFP[:], AF.Ln, bias=1.0)
            abuf = k.sb("ml_abuf", [128, BW], F32, s)
            u = k.sb("ml_u", [128, BW], F32, s)
            sgb = k.sb("ml_sgb", [128, BW], F32, s)
            qc = [k.sb(f"ml_qc{i}", [128, T], F32, s) for i in range(2)]
            kc = [k.sb(f"ml_kc{i}", [128, T], F32, s) for i in range(2)]
            yT = k.sb("ml_yT", [128, 2, T], BF16, s)
            ktw = k.sb("ml_ktw", [64, 256], F32, s)
            streams = []
            for d in range(2):
                st = dict(banks=PS[d * 4:(d + 1) * 4])
                for nm, shp in (("sc", [128, 3]), ("uw", [64, 2]), ("DD", [64, 128]), ("dm", [64, 64]), ("rm", [64, 8]),
                                ("Ex", [64, 64]), ("p_", [64, 64]), ("pT", [64, 64]), ("As", [64, 257]), ("NE", [64, 257]),
                                ("ho", [64, 256]), ("sm", [128, 5]), ("sw", [64, 1]), ("ks", [64, 256]), ("Sx", [128, 2, 257]),
                                ("m_", [128, 1])):
                    st[nm] = k.sb(f"ml_{nm}{d}", shp, F32, s)
                st["vx"] = [k.sb(f"ml_vx{d}{i}", [64, 257], F32, s) for i in range(2)]
                st["kt"] = [k.sb(f"ml_kt{d}{i}", [64, 256], F32, s) for i in range(2)]
                for i in range(2):
                    k.memset(st["vx"][i], st["vx"][i][:, 256:257], 1.0)
                streams.append(st)
            fin = dict(hf=[k.sb(f"ml_hf{i}", [64, 256], F32, s) for i in range(2)],
                       hb=[k.sb(f"ml_hb{i}", [64, 256], F32, s) for i in range(2)],
                       oc=[k.sb(f"ml_oc{i}", [64, 256], F32, s) for i in range(2)])
            fst = k.sb("ml_fst", [64, 6], F32, s)
            fmv = k.sb("ml_fmv", [64, 2], F32, s)
            frs = k.sb("ml_frs", [64, 1], F32, s)
            fy = k.sb("ml_fy", [64, 256], F32, s)

            def step(st, h, d, c, it):
                t0 = c * 64
                tsl = slice(t0, t0 + 64)
                B0, B1, B2, B3 = st["banks"]
                sc, uw, DD, dm, rm, Ex, p_, pT, As, NE = (st[n] for n in ("sc", "uw", "DD", "dm", "rm", "Ex", "p_", "pT", "As", "NE"))
                ho, sm, sw, ks, Sx, m_ = (st[n] for n in ("ho", "sm", "sw", "ks", "Sx", "m_"))
                vx, kt = st["vx"][it % 2], st["kt"][it % 2]
                k.dma('sp', vx, vx[:, 0:256], TM["c_v"], TM["c_v"].t[tsl, h * 256:(h + 1) * 256])
                k.dma('sp', kt, kt[:], KTM, KTM.t[tsl, :])
                lfp_col = LFP[:, c, d * 4 + h:d * 4 + h + 1]
                ig_col = G[:, c, d * 8 + h:d * 8 + h + 1]
                i_tri, i_sut, i_msk = (0, 1, 4) if d == 0 else (2, 3, 5)
                k.mm(B0, B0[0:64, 0:1], MC, MC[:, i_tri, :], LFP, lfp_col, True, True)
                k.mm(B0, B0[0:64, 1:2], MC, MC[:, i_sut, :], LFP, lfp_col, True, True)
                k.mm(B0, B0[:, 2:3], MC2, MC2[:, 0, :], LFP, lfp_col, True, True)
                k.copy(sc, sc[0:64, 0:2], B0, B0[0:64, 0:2], eng='act')
                k.copy(sc, sc[:, 2:3], B0, B0[:, 2:3], eng='act')
                k.tt(uw, uw[:, 0:1], G, ig_col, sc, sc[0:64, 0:1], ALU.subtract)
                k.tt(uw, uw[:, 1:2], G, ig_col, sc, sc[0:64, 1:2], ALU.add)
                k.ts(DD, DD[:, 0:64], ident_f, ident_f[0:64, 0:64], uw[:, 0:1], None, ALU.mult, extra_reads=[uw])
                k.ts(DD, DD[:, 64:128], ident_f, ident_f[0:64, 0:64], uw[:, 1:2], None, ALU.mult, extra_reads=[uw])
                k.mm(B0, B0[:, 64:192], MC2, MC2[:, 1, :], DD, DD[:], True, True)
                k.stt(dm, dm[:], B0, B0[0:64, 64:128], sc[0:64, 0:1], MC, MC[:, i_msk, :], ALU.add, ALU.add, extra_reads=[sc])
                k.op('dve', [dm], [rm], lambda e: e.reduce_max(rm[:, 0:1], dm[:], AX.X))
                k.tt(rm, rm[:, 1:2], sc, sc[0:64, 0:1], m_, m_[0:64, 0:1], ALU.add)
                k.tt(rm, rm[:, 2:3], rm, rm[:, 0:1], rm, rm[:, 1:2], ALU.max)
                k.ts(rm, rm[:, 3:4], rm, rm[:, 2:3], -1.0, None, ALU.mult)
                k.act(Ex, Ex[:], dm, dm[:], AF.Exp, bias=rm[:, 3:4], extra_reads=[rm])
                k.mm(B0, B0[0:64, 192:256], qc[0], qc[0][:, tsl], kc[0], kc[0][:, tsl], True, False)
                k.mm(B0, B0[0:64, 192:256], qc[1], qc[1][:, tsl], kc[1], kc[1][:, tsl], False, True)
                k.tt(p_, p_[:], B0, B0[0:64, 192:256], Ex, Ex[:], ALU.mult)
                k.tr(B0, B0[0:64, 256:320], p_, p_[:], ident_f, ident_f[0:64, 0:64])
                k.copy(pT, pT[:], B0, B0[0:64, 256:320], eng='act')
                k.mm(B1, B1[0:64, 0:257], pT, pT[:], vx, vx[:], True, True)
                k.mm(B2, B2[0:64, 0:257], qc[0], qc[0][:, tsl], Sx, Sx[:, 0, :], True, False)
                k.mm(B2, B2[0:64, 0:257], qc[1], qc[1][:, tsl], Sx, Sx[:, 1, :], False, True)
                k.act(rm, rm[:, 4:5], rm, rm[:, 1:2], AF.Exp, bias=rm[:, 3:4])
                k.copy(As, As[:], B1, B1[0:64, 0:257], eng='act')
                k.stt(NE, NE[:], B2, B2[0:64, 0:257], rm[:, 4:5], As, As[:], ALU.mult, ALU.add, extra_reads=[rm])
                k.act(rm, rm[:, 5:6], rm, rm[:, 2:3], AF.Exp, scale=-1.0)
                k.ts(rm, rm[:, 6:7], NE, NE[:, 256:257], -1.0, None, ALU.mult)
                k.tt(rm, rm[:, 6:7], rm, rm[:, 6:7], NE, NE[:, 256:257], ALU.max)
                k.tt(rm, rm[:, 6:7], rm, rm[:, 6:7], rm, rm[:, 5:6], ALU.max)
                k.op('dve', [rm], [rm], lambda e: e.reciprocal(rm[:, 7:8], rm[:, 6:7]))
                k.ts(ho, ho[:], NE, NE[:, 0:256], rm[:, 7:8], None, ALU.mult, extra_reads=[rm])
                k.dma('sp', OM, OM.t[d, tsl, :], ho, ho[:])
                k.op('dve', [B0], [sm], lambda e: e.reduce_max(sm[:, 0:1], B0[:, 128:192], AX.X))
                k.tt(sm, sm[:, 1:2], sc, sc[:, 2:3], m_, m_[:, 0:1], ALU.add)
                k.tt(sm, sm[:, 2:3], sm, sm[:, 0:1], sm, sm[:, 1:2], ALU.max)
                k.ts(sm, sm[:, 3:4], sm, sm[:, 2:3], -1.0, None, ALU.mult)
                k.act(sw, sw[:], uw, uw[:, 1:2], AF.Exp, bias=sm[0:64, 3:4], extra_reads=[sm])
                k.act(sm, sm[:, 4:5], sm, sm[:, 1:2], AF.Exp, bias=sm[:, 3:4])
                k.ts(ks, ks[:], kt, kt[:], sw[:, 0:1], None, ALU.mult, extra_reads=[sw])
                for dc in range(2):
                    k.mm(B3, B3[:, 0:257], ks, ks[:, dc * 128:(dc + 1) * 128], vx, vx[:], True, True)
                    k.stt(Sx, Sx[:, dc, :], Sx, Sx[:, dc, :], sm[:, 4:5], B3, B3[:, 0:257], ALU.mult, ALU.add, extra_reads=[sm])
                k.copy(m_, m_[:], sm, sm[:, 2:3])

            for h in range(4):
                for blk in range(4):
                    which, dc = divmod(blk, 2)
                    src = FT["c_q"] if which == 0 else FT["c_k"]
                    dstt = (qc if which == 0 else kc)[dc]
                    ch0 = h * 256 + dc * 128
                    pj = which * 8 + h * 2 + dc
                    k.memset(abuf, abuf[:], 0.0)
                    k.dma('sp', abuf, abuf[:, LAT0:LAT0 + NL], src, src.t[ch0:ch0 + 128, 0:NL])
                    k.dma('act', abuf, abuf[:, CTX0:CTX0 + NCTX], src, src.t[ch0:ch0 + 128, NL:T])
                    lo, hi = 1, BW - 3
                    k.ts(u, u[:, lo:hi], abuf, abuf[:, lo - 1:hi - 1], cpar[:, pj, 0:1], cpar[:, pj, 4:5], ALU.mult, ALU.add,
                         extra_reads=[cpar])
                    for kk in range(1, 4):
                        k.stt(u, u[:, lo:hi], abuf, abuf[:, lo - 1 + kk:hi - 1 + kk], cpar[:, pj, kk:kk + 1], u, u[:, lo:hi],
                              ALU.mult, ALU.add, extra_reads=[cpar])
                    k.act(sgb, sgb[:, lo:hi], u, u[:, lo:hi], AF.Sigmoid)
                    scl = 1.0 if which == 0 else 256 ** -0.5
                    k.stt(dstt, dstt[:, 0:NL], u, u[:, LAT0:LAT0 + NL], scl, sgb, sgb[:, LAT0:LAT0 + NL], ALU.mult, ALU.mult)
                    k.stt(dstt, dstt[:, NL:T], u, u[:, CTX0:CTX0 + NCTX], scl, sgb, sgb[:, CTX0:CTX0 + NCTX], ALU.mult, ALU.mult)
                for c in range(68):
                    tsl = slice(c * 64, c * 64 + 64)
                    pt = PS[c % 2]
                    for dc in range(2):
                        k.tr(pt, pt[0:64, dc * 128:(dc + 1) * 128], kc[dc], kc[dc][:, tsl], ident_f, ident_f[:])
                    k.copy(ktw, ktw[:], pt, pt[0:64, 0:256], eng='act')
                    k.dma('sp', KTM, KTM.t[tsl, :], ktw, ktw[:])
                for d in range(2):
                    k.memset(streams[d]["Sx"], streams[d]["Sx"][:], 0.0)
                    k.memset(streams[d]["m_"], streams[d]["m_"][:], 0.0)
                for sidx in range(68):
                    step(streams[0], h, 0, CH_F[sidx], sidx)
                    step(streams[1], h, 1, CH_B[sidx], sidx)
                for c in range(68):
                    tsl = slice(c * 64, c * 64 + 64)
                    hf, hb, oc = fin["hf"][c % 2], fin["hb"][c % 2], fin["oc"][c % 2]
                    k.dma('sp', hf, hf[:], OM, OM.t[0, tsl, :])
                    k.dma('act', hb, hb[:], OM, OM.t[1, tsl, :])
                    k.dma('sp', oc, oc[:], TM["c_o"], TM["c_o"].t[tsl, h * 256:(h + 1) * 256])
                    k.tt(hf, hf[:], hf, hf[:], hb, hb[:], ALU.add)
                    k.op('dve', [hf], [fst], lambda e: e.bn_stats(fst[:], hf[:]))
                    k.op('dve', [fst], [fmv], lambda e: e.bn_aggr(fmv[:], fst[:]))
                    k.act(frs, frs[:], fmv, fmv[:, 1:2], AF.Ln, bias=EPS)
                    k.act(frs, frs[:], frs, frs[:], AF.Exp, scale=-0.5)
                    k.ts(hf, hf[:], hf, hf[:], fmv[:, 0:1], frs[:, 0:1], ALU.subtract, ALU.mult, extra_reads=[fmv, frs])
                    k.tt(hf, hf[:], hf, hf[:], ng, ng[:, h * 256:(h + 1) * 256], ALU.mult)
                    k.act(oc, oc[:], oc, oc[:], AF.Sigmoid)
                    k.tt(fy, fy[:], hf, hf[:], oc, oc[:], ALU.mult)
                    pt = PS[2 + c % 2]
                    for hf2 in range(2):
                        k.tr(pt, pt[:, hf2 * 64:(hf2 + 1) * 64], fy, fy[:, hf2 * 128:(hf2 + 1) * 128], ident_f, ident_f[0:64, 0:64])
                    k.copy(yT, yT[:, :, tsl], pt, pt[:, 0:128].rearrange("p (a t) -> p a t", a=2), eng='act')
                for hf2 in range(2):
                    k.dma('sp', YT, YT.t[2, h * 2 + hf2, :, :], yT, yT[:, hf2, :])

    w_br_s = [ein(f"w_br{l}", [384, D]) for l in range(NLAY)]
    w_out_s = [ein(f"w_out{l}", [512, D]) for l in range(NLAY)]
    b_gT = [ein(f"b_gT{l}", [128, 96]) for l in range(NLAY)]
    lnp = [ein(f"lnp{l}", [4, D]) for l in range(NLAY)]
    Zd = k.dram("Zd", [T, D], F32)
    ALPHA = (2 * DEPTH) ** 0.25

    def ln_apply(l, which, Xout, MV, h2=None):
        with ExitStack() as s:
            gb = k.sb("ln_g", [128, D], F32, s)
            bb = k.sb("ln_b", [128, D], F32, s)
            k.dma('sp', gb, gb[:], lnp[l], lnp[l].t[which * 2, :].partition_broadcast(128))
            k.dma('act', bb, bb[:], lnp[l], lnp[l].t[which * 2 + 1, :].partition_broadcast(128))
            zt = [k.sb(f"ln_z{i}", [128, D], F32, s) for i in range(2)]
            for i in range(T // 128):
                z = zt[i % 2]
                k.dma('sp' if i % 2 == 0 else 'act', z, z[:], Zd, Zd.t[i * 128:(i + 1) * 128, :])
                k.ts(z, z[:], z, z[:], MV[:, i, 0:1], MV[:, i, 1:2], ALU.subtract, ALU.mult, extra_reads=[MV])
                k.tt(z, z[:], z, z[:], gb, gb[:], ALU.mult)
                k.tt(z, z[:], z, z[:], bb, bb[:], ALU.add)
                k.dma('sp', Xout, Xout.t[i * 128:(i + 1) * 128, :], z, z[:])

    def finish_stats(STb, MV, i):
        k.op('dve', [STb], [MV], lambda e: e.bn_aggr(MV[:, i, :], STb[:]))
        k.act(MV, MV[:, i, 1:2], MV, MV[:, i, 1:2], AF.Ln, bias=EPS)
        k.act(MV, MV[:, i, 1:2], MV, MV[:, i, 1:2], AF.Exp, scale=-0.5)

    def stage_C(l, X, MV):
        wg = Win[l].pieces[0]
        wb = Wbr[l].pieces[0]
        wo = Wout[l].pieces[0]
        with ExitStack() as s:
            h = k.sb("C_h", [128, 32, 512], BF16, s)
            yb_ = k.sb("C_y", [128, 24, 512], BF16, s)
            MT = k.sb("C_MT", [128, 32, 512], BF16, s)
            wsl = [k.sb(f"C_w{i}", [128, 32, 256], BF16, s) for i in range(2)]
            bsl = [k.sb(f"C_wb{i}", [128, 8, 256], BF16, s) for i in range(2)]
            g1 = k.sb("C_g1", [128, D], F32, s)
            gbT = k.sb("C_gbT", [128, 96], F32, s)
            k.dma('sp', gbT, gbT[:], b_gT[l], b_gT[l].t[:, :])
            gs = [k.sb(f"C_gs{i}", [128, 512], F32, s) for i in range(2)]
            acc = [k.sb(f"C_acc{i}", [128, 512], F32, s) for i in range(2)]
            tmp = k.sb("C_tmp", [128, 512], F32, s)
            xt = [k.sb(f"C_x{i}", [128, 256], F32, s) for i in range(2)]
            zt = [k.sb(f"C_z{i}", [128, 256], F32, s) for i in range(2)]
            STb = [k.sb(f"C_st{i}", [128, 16, 6], F32, s) for i in range(4)]
            nw = 0
            npz = 0
            nx = 0
            for blk in range(NBLK):
                N = 512 if blk < 8 else 256
                t0 = blk * 512
                mrow = 0 if blk < 8 else 1
                if blk in (0, 8):
                    k.dma('sp', g1, g1[:], MODP[l], MODP[l].t[mrow, 2 * D:3 * D].partition_broadcast(128))
                k.dma('sp', h, h[:, :, 0:N], HT[blk], HT[blk].t[:, :, 0:N])
                k.dma('act', yb_, yb_[:, :, 0:N], YT, YT.t[:, :, :, t0:t0 + N].rearrange("n c p t -> p (n c) t"))
                for mg in range(16):
                    for n in range(3):
                        w = wsl[nw % 2]
                        b = bsl[nw % 2]
                        nw += 1
                        c0 = N_FEAT + n * D + mg * 256
                        k.dma('act', w, w[:], wg, wg.t[:, c0:c0 + 256].rearrange("(k p) c -> p k c", p=128))
                        k.dma('sp', b, b[:], wb, wb.t[n * 1024:(n + 1) * 1024, mg * 256:(mg + 1) * 256].rearrange("(k p) c -> p k c", p=128))
                        for mm in range(2):
                            m = mg * 2 + mm
                            pg, py = PS[npz % 8], PS[(npz + 1) % 8]
                            npz += 2
                            for kc in range(32):
                                k.mm(pg, pg[:, 0:N], w, w[:, kc, mm * 128:(mm + 1) * 128], h, h[:, kc, 0:N], kc == 0, kc == 31)
                            for kc in range(8):
                                k.mm(py, py[:, 0:N], b, b[:, kc, mm * 128:(mm + 1) * 128], yb_, yb_[:, n * 8 + kc, 0:N], kc == 0, kc == 7)
                            g_ = gs[mm]
                            k.act(g_, g_[:, 0:N], pg, pg[:, 0:N], AF.Sigmoid, bias=gbT[:, n * 32 + m:n * 32 + m + 1], extra_reads=[gbT])
                            a_ = acc[mm]
                            if n == 0:
                                k.tt(a_, a_[:, 0:N], py, py[:, 0:N], g_, g_[:, 0:N], ALU.mult)
                            elif n == 1:
                                k.tt(tmp, tmp[:, 0:N], py, py[:, 0:N], g_, g_[:, 0:N], ALU.mult)
                                k.tt(a_, a_[:, 0:N], a_, a_[:, 0:N], tmp, tmp[:, 0:N], ALU.add)
                            else:
                                k.tt(tmp, tmp[:, 0:N], py, py[:, 0:N], g_, g_[:, 0:N], ALU.mult)
                                k.tt(MT, MT[:, m, 0:N], a_, a_[:, 0:N], tmp, tmp[:, 0:N], ALU.add)
                ntt = N // 128
                for cbs in range(16):
                    w = wsl[nw % 2]
                    nw += 1
                    k.dma('act', w, w[:], wo, wo.t[:, cbs * 256:(cbs + 1) * 256].rearrange("(k p) c -> p k c", p=128))
                    for tt in range(ntt):
                        ps = PS[npz % 8]
                        npz += 1
                        for kc in range(32):
                            k.mm(ps, ps[:, 0:256], MT, MT[:, kc, tt * 128:(tt + 1) * 128], w, w[:, kc, :], kc == 0, kc == 31)
                        x_, z_ = xt[nx % 2], zt[nx % 2]
                        nx += 1
                        r0 = t0 + tt * 128
                        k.dma('sp', x_, x_[:], X, X.t[r0:r0 + 128, cbs * 256:(cbs + 1) * 256])
                        k.tt(z_, z_[:], ps, ps[:, 0:256], g1, g1[:, cbs * 256:(cbs + 1) * 256], ALU.mult)
                        k.stt(z_, z_[:], x_, x_[:], ALPHA, z_, z_[:], ALU.mult, ALU.add)
                        k.op('dve', [z_], [STb[tt]], lambda e: e.bn_stats(STb[tt][:, cbs, :], z_[:]))
                        k.dma('sp', Zd, Zd.t[r0:r0 + 128, cbs * 256:(cbs + 1) * 256], z_, z_[:])
                for tt in range(ntt):
                    finish_stats(STb[tt], MV, blk * 4 + tt)

    w_rt = [ein(f"w_rt{l}", [128, 32, 16]) for l in range(NLAY)]
    w_eg_s = [ein(f"w_eg{l}", [8192, 1024]) for l in range(NLAY)]
    w_eu_s = [ein(f"w_eu{l}", [8192, 1024]) for l in range(NLAY)]
    w_ed_s = [ein(f"w_ed{l}", [2048, D]) for l in range(NLAY)]
    H2 = k.dram("H2", [T + 128, D], BF16)
    dummy_idx = ein("dummy_idx", [128, 1], U32)
    Fd = [k.dram(f"Fd{i}", [T + 128, 2048], F32) for i in range(2)]
    NSLOT = 544
    SLOT_TILES = [(0, 128), (128, 128), (256, 128), (384, 128), (512, 32)]

    def stage_E1(l, IDXT, GVT):
        with ExitStack() as s:
            wrf = k.sb("E_wrf", [128, 32, 16], F32, s)
            wr = k.sb("E_wr", [128, 32, 16], BF16, s)
            k.dma('sp', wrf, wrf[:], w_rt[l], w_rt[l].t[:, :, :])
            k.copy(wr, wr[:], wrf, wrf[:])
            AFFT = k.sb("E_afft", [16, T], F32, s)
            hb = [k.sb(f"E_h{i}", [128, 32, 512], BF16, s) for i in range(2)]
            sm = k.sb("E_sm", [128, 4], F32, s)
            ex = k.sb("E_ex", [128, 16], F32, s)
            for blk in range(NBLK):
                N = 512 if blk < 8 else 256
                h = hb[blk % 2]
                k.dma('sp', h, h[:, :, 0:N], HT[blk], HT[blk].t[:, :, 0:N])
                for tt in range(N // 128):
                    ps, pt = PS[(blk * 4 + tt) % 4], PS[4 + (blk * 4 + tt) % 4]
                    for kc in range(32):
                        k.mm(ps, ps[:, 0:16], h, h[:, kc, tt * 128:(tt + 1) * 128], wr, wr[:, kc, :], kc == 0, kc == 31)
                    k.op('dve', [ps], [sm], lambda e: e.reduce_max(sm[:, 0:1], ps[:, 0:16], AX.X))
                    k.ts(sm, sm[:, 1:2], sm, sm[:, 0:1], -1.0, None, ALU.mult)
                    k.act(ex, ex[:], ps, ps[:, 0:16], AF.Exp, bias=sm[:, 1:2], extra_reads=[sm], accum=sm, accum_ap=sm[:, 2:3])
                    k.op('dve', [sm], [sm], lambda e: e.reciprocal(sm[:, 3:4], sm[:, 2:3]))
                    k.ts(ex, ex[:], ex, ex[:], sm[:, 3:4], None, ALU.mult, extra_reads=[sm])
                    k.tr(pt, pt[0:16, 0:128], ex, ex[:], ident_f, ident_f[:])
                    c0 = blk * 512 + tt * 128
                    k.copy(AFFT, AFFT[:, c0:c0 + 128], pt, pt[0:16, 0:128], eng='act')
            work = k.sb("E_work", [16, NL], F32, s)
            workc = k.sb("E_workc", [16, NCTX], F32, s)
            GV = k.sb("E_gv", [16, NSLOT], F32, s)
            IDX = k.sb("E_idx", [16, NSLOT], U32, s)
            IDXF = k.sb("E_idxf", [16, NSLOT], F32, s)
            k.copy(work, work[:], AFFT, AFFT[:, 0:NL])
            k.copy(workc, workc[:], AFFT, AFFT[:, NL:T])
            for wk, base, rounds in ((work, 0, 64), (workc, 512, 4)):
                for r in range(rounds):
                    sl = slice(base + r * 8, base + r * 8 + 8)
                    k.op('dve', [wk], [GV], lambda e: e.max(GV[:, sl], wk[:]))
                    k.op('dve', [wk, GV], [IDX], lambda e: e.max_index(IDX[:, sl], GV[:, sl], wk[:]))
                    k.op('dve', [wk, GV], [wk], lambda e: e.match_replace(wk[:], GV[:, sl], wk[:], -1.0))
            k.copy(IDXF, IDXF[:], IDX, IDX[:])
            k.ts(IDXF, IDXF[:, 512:544], IDXF, IDXF[:, 512:544], float(NL), None, ALU.add)
            idf = k.sb("E_idf", [128, 16], F32, s)
            k.memset(GVT, GVT[:], 0.0)
            dmy = k.sb("E_dmy", [128, 1], U32, s)
            k.dma('sp', dmy, dmy[:], dummy_idx, dummy_idx.t[:, :])
            for st, (s0, n) in enumerate(SLOT_TILES):
                pt = PS[st % 4]
                k.tr(pt, pt[0:n, 0:16], IDXF, IDXF[:, s0:s0 + n], ident_f, ident_f[0:16, 0:16])
                k.tr(pt, pt[0:n, 16:32], GV, GV[:, s0:s0 + n], ident_f, ident_f[0:16, 0:16])
                for e in range(16):
                    it_ = IDXT[st][e]
                    if n < 128:
                        k.copy(it_, it_[:], dmy, dmy[:])
                    k.copy(it_, it_[0:n, :], pt, pt[0:n, e:e + 1])
                k.copy(GVT, GVT[0:n, st, :], pt, pt[0:n, 16:32], eng='act')

    def stage_E2(l, IDXT, GVT):
        wge, wue, wde = Wge[l].pieces[0], Wue[l].pieces[0], Wde[l].pieces[0]
        with ExitStack() as s:
            xg = [k.sb(f"E_xg{i}", [128, D], BF16, s) for i in range(2)]
            xgT = k.sb("E_xgT", [128, 32, NSLOT], BF16, s)
            hidT = k.sb("E_hidT", [128, 8, NSLOT], BF16, s)
            wsl = [k.sb(f"E_w{i}", [128, 32, 256], BF16, s) for i in range(3)]
            dsl = [k.sb(f"E_wd{i}", [128, 8, 512], BF16, s) for i in range(2)]
            yrow = [k.sb(f"E_y{i}", [128, D], F32, s) for i in range(2)]
            sg = [k.sb(f"E_sg{i}", [128, 272], F32, s) for i in range(2)]
            tg = [k.sb(f"E_tg{i}", [128, 272], F32, s) for i in range(2)]
            k.memset(yrow[0], yrow[0][:], 0.0)
            for i in range(T // 128 + 1):
                for hf in range(2):
                    k.dma('sp', Fd[hf], Fd[hf].t[i * 128:(i + 1) * 128, :], yrow[0], yrow[0][:, hf * 2048:(hf + 1) * 2048])
            nxg = nw = nd = ny = npz = 0
            for e in range(16):
                for st, (s0, n) in enumerate(SLOT_TILES):
                    g = xg[nxg % 2]
                    nxg += 1
                    it_ = IDXT[st][e]
                    k._waits('pool', [H2, it_], [g])
                    ds = k._dsem(g)
                    ins = nc.gpsimd.indirect_dma_start(out=g[:, :], out_offset=None, in_=H2.t[:, :],
                                                       in_offset=bass.IndirectOffsetOnAxis(ap=it_[:, 0:1], axis=0))
                    ds.cnt += 16
                    ins.then_inc(ds.sem, 16)
                    tok = ('d', ds.id, ds.cnt)
                    H2.r[('d', ds.id)] = tok
                    it_.r[('d', ds.id)] = tok
                    g.w = tok
                    g.r = {}
                    for gq in range(4):
                        pb = PS[npz % 8]
                        npz += 1
                        pbv = pb.t[:].bitcast(BF16)
                        for j in range(8):
                            kc = gq * 8 + j
                            k.tr(pb, pbv[:, j * 128:j * 128 + n], g, g[0:n, kc * 128:(kc + 1) * 128], ident_bf, ident_bf[0:n, 0:n])
                        src_ap = pbv.rearrange("p (j t) -> p j t", j=8)[:, :, 0:n]
                        k.copy(xgT, xgT[:, gq * 8:(gq + 1) * 8, s0:s0 + n], pb, src_ap, eng=('act' if gq % 2 == 0 else 'dve'))
                for mg in range(4):
                    wg_, wu_ = wsl[nw % 3], wsl[(nw + 1) % 3]
                    nw += 2
                    k.dma('sp', wg_, wg_[:], wge, wge.t[e * 4096:(e + 1) * 4096, mg * 256:(mg + 1) * 256].rearrange("(k p) c -> p k c", p=128))
                    k.dma('act', wu_, wu_[:], wue, wue.t[e * 4096:(e + 1) * 4096, mg * 256:(mg + 1) * 256].rearrange("(k p) c -> p k c", p=128))
                    for mm in range(2):
                        m = mg * 2 + mm
                        for half in range(2):
                            cs = slice(half * 272, (half + 1) * 272)
                            pg, pu = PS[npz % 8], PS[(npz + 1) % 8]
                            npz += 2
                            for kc in range(32):
                                k.mm(pg, pg[:, 0:272], wg_, wg_[:, kc, mm * 128:(mm + 1) * 128], xgT, xgT[:, kc, cs], kc == 0, kc == 31)
                            for kc in range(32):
                                k.mm(pu, pu[:, 0:272], wu_, wu_[:, kc, mm * 128:(mm + 1) * 128], xgT, xgT[:, kc, cs], kc == 0, kc == 31)
                            s_, t_ = sg[half], tg[half]
                            k.act(s_, s_[:], pg, pg[:, 0:272], AF.Sigmoid)
                            k.tt(t_, t_[:], pg, pg[:, 0:272], s_, s_[:], ALU.mult)
                            k.tt(hidT, hidT[:, m, cs], pu, pu[:, 0:272], t_, t_[:], ALU.mult)
                for st, (s0, n) in enumerate(SLOT_TILES):
                    y = yrow[ny % 2]
                    ny += 1
                    for cb in range(8):
                        wd_ = dsl[nd % 2]
                        nd += 1
                        k.dma('sp' if cb % 2 == 0 else 'act', wd_, wd_[:], wde,
                              wde.t[e * 1024:(e + 1) * 1024, cb * 512:(cb + 1) * 512].rearrange("(k p) c -> p k c", p=128))
                        ps = PS[npz % 8]
                        npz += 1
                        for kc in range(8):
                            k.mm(ps, ps[0:n, :], hidT, hidT[:, kc, s0:s0 + n], wd_, wd_[:, kc, :], kc == 0, kc == 7)
                        if n < 128 and cb == 0:
                            k.memset(y, y[:], 0.0)
                        k.ts(y, y[0:n, cb * 512:(cb + 1) * 512], ps, ps[0:n, :], GVT[0:n, st, e:e + 1], None, ALU.mult, extra_reads=[GVT])
                    it_ = IDXT[st][e]
                    for hf in range(2):
                        F_ = Fd[hf]
                        k._waits('pool', [y, it_], [F_])
                        ds = k._dsem(F_)
                        ins = nc.gpsimd.indirect_dma_start(out=F_.t[:, :], out_offset=bass.IndirectOffsetOnAxis(ap=it_[:, 0:1], axis=0),
                                                           in_=y[:, hf * 2048:(hf + 1) * 2048], in_offset=None, compute_op=ALU.add)
                        ds.cnt += 16
                        ins.then_inc(ds.sem, 16)
                        tok = ('d', ds.id, ds.cnt)
                        y.r[('d', ds.id)] = tok
                        it_.r[('d', ds.id)] = tok
                        F_.w = tok
                        F_.r = {}

    def stage_E3(l, Xin, Xout):
        with ExitStack() as s:
            gb = k.sb("L2_g", [128, D], F32, s)
            bb = k.sb("L2_b", [128, D], F32, s)
            g2 = k.sb("L2_g2", [128, D], F32, s)
            k.dma('sp', gb, gb[:], lnp[l], lnp[l].t[2, :].partition_broadcast(128))
            k.dma('act', bb, bb[:], lnp[l], lnp[l].t[3, :].partition_broadcast(128))
            xt = [k.sb(f"L2_x{i}", [128, D], F32, s) for i in range(2)]
            ft = [k.sb(f"L2_f{i}", [128, D], F32, s) for i in range(2)]
            stt_ = k.sb("L2_st", [128, 8, 6], F32, s)
            mv = k.sb("L2_mv", [128, 2], F32, s)
            for i in range(T // 128):
                if i in (0, NL // 128):
                    k.dma('sp', g2, g2[:], MODP[l], MODP[l].t[0 if i == 0 else 1, 5 * D:6 * D].partition_broadcast(128))
                x_, f_ = xt[i % 2], ft[i % 2]
                k.dma('sp', x_, x_[:], Xin, Xin.t[i * 128:(i + 1) * 128, :])
                for hf in range(2):
                    k.dma('act', f_, f_[:, hf * 2048:(hf + 1) * 2048], Fd[hf], Fd[hf].t[i * 128:(i + 1) * 128, :])
                k.tt(f_, f_[:], f_, f_[:], g2, g2[:], ALU.mult)
                k.stt(x_, x_[:], x_, x_[:], ALPHA, f_, f_[:], ALU.mult, ALU.add)
                for c in range(8):
                    k.op('dve', [x_], [stt_], lambda e: e.bn_stats(stt_[:, c, :], x_[:, c * 512:(c + 1) * 512]))
                k.op('dve', [stt_], [mv], lambda e: e.bn_aggr(mv[:], stt_[:]))
                k.act(mv, mv[:, 1:2], mv, mv[:, 1:2], AF.Ln, bias=EPS)
                k.act(mv, mv[:, 1:2], mv, mv[:, 1:2], AF.Exp, scale=-0.5)
                k.ts(x_, x_[:], x_, x_[:], mv[:, 0:1], mv[:, 1:2], ALU.subtract, ALU.mult, extra_reads=[mv])
                k.tt(x_, x_[:], x_, x_[:], gb, gb[:], ALU.mult)
                k.tt(x_, x_[:], x_, x_[:], bb, bb[:], ALU.add)
                k.dma('sp', Xout, Xout.t[i * 128:(i + 1) * 128, :], x_, x_[:])

    if 'C' in stages:
        for l in range(NLAY):
            Wbr.append(gather_weight(k, f"wbr{l}", w_br_s[l], 384, D, BF16, 1, cast=True))
            Wout.append(gather_weight(k, f"wout{l}", w_out_s[l], 512, D, BF16, 1, cast=True))
    Wge, Wue, Wde = [], [], []
    if 'E' in stages:
        for l in range(NLAY):
            Wge.append(gather_weight(k, f"wge{l}", w_eg_s[l], 8192, 1024, BF16, 1, cast=True))
            Wue.append(gather_weight(k, f"wue{l}", w_eu_s[l], 8192, 1024, BF16, 1, cast=True))
            Wde.append(gather_weight(k, f"wde{l}", w_ed_s[l], 2048, D, BF16, 1, cast=True))
    IDXT = [[k.sb(f"IDXT{st}_{e}", [128, 1], U32) for e in range(16)] for st in range(5)]
    GVT = k.sb("GVT", [128, 5, 16], F32)
    X2 = [k.dram(f"X2_{l}", [T, D], F32) for l in range(NLAY)]
    MV1 = k.sb("MV1", [128, 34, 2], F32)
    X1 = [k.dram(f"X1_{l}", [T, D], F32) for l in range(NLAY)]
    X = X0
    for l in range(NLAY):
        if 'mod' in stages:
            stage_mod(l)
            k.barrier()
        if 'A' in stages:
            stage_A(l, X, 0)
            k.barrier()
        if 'B' in stages:
            stage_B(l)
            k.barrier()
        if 'R' in stages:
            stage_R(l)
            k.barrier()
        if 'G' in stages:
            stage_G(l)
            k.barrier()
        if 'M' in stages:
            stage_M(l)
            k.barrier()
        if 'C' in stages:
            stage_C(l, X, MV1)
            k.barrier()
            ln_apply(l, 0, X1[l], MV1)
            k.barrier()
        if 'E' in stages:
            stage_A(l, X1[l], 3, Hout=H2)
            k.barrier()
            stage_E1(l, IDXT, GVT)
            k.barrier()
            stage_E2(l, IDXT, GVT)
            k.barrier()
            stage_E3(l, X1[l], X2[l])
            k.barrier()
            X = X2[l]

    final = []
    for name, src, shape, dt in dbg(locals()):
        o = k.dram("out_" + name, list(shape), dt, kind="ExternalOutput")
        nrow = shape[0]
        for r0 in range(0, nrow, 1024):
            r1 = min(nrow, r0 + 1024)
            k.dma('sp', o, o.t[r0:r1], src, src.t[r0:r1])
        final.append(o)
    k.wait_all('sp', final)
    es.close()
    return nc, k


def host_inputs(inputs, stages):
    NLAY = DEPTH if 'l1' in stages else 1
    maps = []
    ident = np.eye(128, dtype=np.float32)
    for core in range(NCORES):
        smp = core // 2
        m = {}
        m["xin"] = np.ascontiguousarray(np.concatenate([inputs["x"][smp], inputs["ctx"][smp]], axis=0))
        c2 = np.stack([inputs["c"][smp], inputs["c_ctx"]], axis=0)
        m["cT"] = np.ascontiguousarray(c2.reshape(2, 32, 128).transpose(2, 1, 0))
        m["ident"] = ident
        m["dummy_idx"] = (T + np.arange(128, dtype=np.uint32))[:, None]
        jj, ii = np.meshgrid(np.arange(64), np.arange(64), indexing="ij")
        gc = np.zeros((64, 6, 64), np.float32)
        gc[:, 0] = (jj <= ii); gc[:, 1] = (jj <= ii) / -16.0; gc[:, 2] = (jj > ii) / -16.0
        gc[:, 3] = (jj >= ii); gc[:, 4] = (jj >= ii) / -16.0; gc[:, 5] = (jj < ii) / -16.0
        m["gla_c"] = gc
        mc = np.zeros((64, 6, 64), np.float32)
        mc[:, 0] = (jj <= ii) * -1.0; mc[:, 1] = (jj > ii) * -1.0; mc[:, 2] = (jj >= ii) * -1.0; mc[:, 3] = (jj < ii) * -1.0
        mc[:, 4] = np.where(ii <= jj, 0.0, -1e30)
        mc[:, 5] = np.where(ii >= jj, 0.0, -1e30)
        m["ml_c"] = mc
        mc2 = np.zeros((64, 2, 128), np.float32); mc2[:, 0] = -1.0; mc2[:, 1] = 1.0
        m["ml_c2"] = mc2
        for l in range(NLAY):
            m[f"w_mod{l}"] = np.ascontiguousarray(inputs["w_mod"][l][core * 512:(core + 1) * 512])
            m[f"b_mod{l}"] = np.ascontiguousarray(inputs["b_mod"][l][None, :])
            m[f"w_in{l}"] = np.ascontiguousarray(inputs["w_in"][l][core * 512:(core + 1) * 512])
            m[f"b_in{l}"] = np.ascontiguousarray(inputs["b_in"][l][None, :])
            m[f"b_inF{l}"] = host_bias_F(inputs["b_in"][l])
            if "w_router" in inputs:
                m[f"w_rt{l}"] = np.ascontiguousarray(inputs["w_router"][l].reshape(32, 128, 16).transpose(1, 0, 2))
                m[f"w_eg{l}"] = np.ascontiguousarray(inputs["w_e_gate"][l][2 * core:2 * core + 2].reshape(8192, 1024))
                m[f"w_eu{l}"] = np.ascontiguousarray(inputs["w_e_up"][l][2 * core:2 * core + 2].reshape(8192, 1024))
                m[f"w_ed{l}"] = np.ascontiguousarray(inputs["w_e_down"][l][2 * core:2 * core + 2].reshape(2048, D))
            if "w_branch" in inputs:
                m[f"w_br{l}"] = np.ascontiguousarray(inputs["w_branch"][l].reshape(3 * 1024, D)[core * 384:(core + 1) * 384])
                m[f"w_out{l}"] = np.ascontiguousarray(inputs["w_out"][l][core * 512:(core + 1) * 512])
                m[f"b_gT{l}"] = np.ascontiguousarray(inputs["b_in"][l][N_FEAT:].reshape(96, 128).T)
                m[f"lnp{l}"] = np.stack([inputs["ln1_g"][l], inputs["ln1_b"][l], inputs["ln2_g"][l], inputs["ln2_b"][l]], axis=0)
            if "conv_c_w" in inputs:
                cp = np.zeros((128, 16, 5), np.float32)
                for kk in range(4):
                    cp[:, :, kk] = inputs["conv_c_w"][l][kk].reshape(16, 128).T
                cp[:, :, 4] = inputs["conv_c_b"][l].reshape(16, 128).T
                m[f"mlc_par{l}"] = cp
                m[f"ml_ng{l}"] = np.ascontiguousarray(inputs["mlstm_norm_g"][l][None])
            if "gla_wa2" in inputs:
                m[f"gla_wa2_{l}"] = np.ascontiguousarray(inputs["gla_wa2"][l].transpose(1, 0, 2))
                m[f"gla_ba_{l}"] = np.ascontiguousarray(inputs["gla_ba"][l][None])
                m[f"gla_ng_{l}"] = np.ascontiguousarray(inputs["gla_norm_g"][l][None])
            if "lru_wa" in inputs:
                par = np.zeros((128, 8, 12), np.float32)
                fm = lambda v: v.reshape(8, 128).T
                for kk in range(4):
                    par[:, :, kk] = fm(inputs["conv_a_w"][l][kk])
                par[:, :, 4] = fm(inputs["conv_a_b"][l])
                for d in range(2):
                    par[:, :, 5 + 2 * d] = fm(inputs["lru_ba"][l][d])
                    par[:, :, 6 + 2 * d] = fm(inputs["lru_bx"][l][d])
                    par[:, :, 9 + d] = fm(inputs["lru_lam"][l][d])
                m[f"lru_par{l}"] = par
                w = np.stack([inputs["lru_wa"][l][0], inputs["lru_wx"][l][0], inputs["lru_wa"][l][1], inputs["lru_wx"][l][1]], axis=1)
                m[f"lru_w{l}"] = np.ascontiguousarray(w.transpose(2, 0, 1, 3))
        maps.append(m)
    return maps


_STAGES = {'mod', 'A', 'B', 'R', 'G', 'M', 'C', 'E', 'l1'}


def _dbg_final(L):
    return [("y", L['X2'][DEPTH - 1], [NL, D], F32)]


def kernel(**inputs):
    inputs = {k_: np.asarray(v) for k_, v in inputs.items()}
    nc, _ = build_program(_STAGES, _dbg_final)
    maps = host_inputs(inputs, _STAGES)
    res = run_bass_kernel_spmd(nc, maps, core_ids=list(range(NCORES)))
    out = np.stack([res.results[2 * s]["out_y"][:NL] for s in range(4)], axis=0)
    return out.astype(np.float32)
```
